# Optimizing a Trainium2 kernel written in Bass

```python
import math
import jax, jax.numpy as jnp
from jax import lax
import numpy as np

D_MODEL = 1024
BATCH = 16
SEQ = 2048
DEPTH = 4

D_MIX = D_MODEL
MLA_HEADS = 8
MLA_NOPE_DIM = 64
MLA_ROPE_DIM = 32
MLA_QK_DIM = MLA_NOPE_DIM + MLA_ROPE_DIM
MLA_V_DIM = 64
MLA_Q_RANK = 256
MLA_KV_RANK = 128
MLA_THETA = 10000.0
DIFF_HEADS = 4
DIFF_HEAD_DIM = 64
DIFF_V_DIM = 2 * DIFF_HEAD_DIM
DIFF_ROT_DIM = DIFF_HEAD_DIM // 4
ROPE_THETA = 500000.0
COL_SIZES = (MLA_Q_RANK, MLA_KV_RANK, MLA_ROPE_DIM,
             DIFF_HEADS * 2 * DIFF_HEAD_DIM, DIFF_HEADS * 2 * DIFF_HEAD_DIM,
             DIFF_HEADS * DIFF_V_DIM)
IN_COLS = sum(COL_SIZES)
MLA_OUT = MLA_HEADS * MLA_V_DIM
DIFF_OUT = DIFF_HEADS * DIFF_V_DIM
D_FF = int(math.ceil(8 * D_MODEL / 3 / 256) * 256)
Q_BLOCK = 128
EPS = 1e-6

kernel_name = "hybrid_mla_diffattn_encoder"


def rms_norm(x, g):
    xf = x.astype(jnp.float32)
    y = xf * lax.rsqrt(jnp.mean(xf * xf, axis=-1, keepdims=True) + EPS)
    return (y * g.astype(jnp.float32)).astype(x.dtype)


def rope(x, positions, theta, rot_dim):
    half = rot_dim // 2
    inv = jnp.exp(-math.log(theta) * jnp.arange(half, dtype=jnp.float32) * (2.0 / rot_dim))
    ang = positions.astype(jnp.float32)[..., None] * inv
    cos = jnp.cos(ang)[:, :, None, :]
    sin = jnp.sin(ang)[:, :, None, :]
    xf = x.astype(jnp.float32)
    x1, x2 = xf[..., :half], xf[..., half:rot_dim]
    out = jnp.concatenate([x1 * cos - x2 * sin, x2 * cos + x1 * sin, xf[..., rot_dim:]], axis=-1)
    return out.astype(x.dtype)


def query_blocks(q):
    B, S, H, D = q.shape
    return jnp.moveaxis(q.reshape(B, S // Q_BLOCK, Q_BLOCK, H, D), 1, 0)


def merge_blocks(o):
    nb, B, QB, H, D = o.shape
    return jnp.moveaxis(o, 0, 1).reshape(B, nb * QB, H, D)


def softmax_map(q, k, scale):
    s = jnp.einsum('bqhd,bkhd->bhqk', q, k).astype(jnp.float32) * scale
    return jax.nn.softmax(s, axis=-1)


def mla_mixer(cq, ckv, k_pe, positions, q_a_norm, w_q_b, kv_a_norm, w_kv_b, q_norm, k_norm):
    B, S, _ = cq.shape
    q = (rms_norm(cq, q_a_norm) @ w_q_b).reshape(B, S, MLA_HEADS, MLA_QK_DIM)
    kv = (rms_norm(ckv, kv_a_norm) @ w_kv_b).reshape(B, S, MLA_HEADS, MLA_NOPE_DIM + MLA_V_DIM)
    q_nope, q_pe = q[..., :MLA_NOPE_DIM], q[..., MLA_NOPE_DIM:]
    k_nope, v = kv[..., :MLA_NOPE_DIM], kv[..., MLA_NOPE_DIM:]
    k_pe = jnp.broadcast_to(k_pe[:, :, None, :], (B, S, MLA_HEADS, MLA_ROPE_DIM))
    q = rope(rms_norm(jnp.concatenate([q_pe, q_nope], -1), q_norm), positions, MLA_THETA, MLA_ROPE_DIM)
    k = rope(rms_norm(jnp.concatenate([k_pe, k_nope], -1), k_norm), positions, MLA_THETA, MLA_ROPE_DIM)
    scale = 1.0 / math.sqrt(MLA_QK_DIM)

    def block(qb):
        a = softmax_map(qb, k, scale).astype(v.dtype)
        return jnp.einsum('bhqk,bkhd->bqhd', a, v)

    o = merge_blocks(lax.map(block, query_blocks(q)))
    return o.reshape(B, S, MLA_OUT)


def diff_mixer(qd, kd, vd, positions, q_norm, k_norm, lq1, lk1, lq2, lk2, subln, lambda_init):
    B, S, _ = qd.shape
    q = rms_norm(qd.reshape(B, S, DIFF_HEADS, 2, DIFF_HEAD_DIM), q_norm)
    k = rms_norm(kd.reshape(B, S, DIFF_HEADS, 2, DIFF_HEAD_DIM), k_norm)
    q = rope(q.reshape(B, S, 2 * DIFF_HEADS, DIFF_HEAD_DIM), positions, ROPE_THETA, DIFF_ROT_DIM)
    k = rope(k.reshape(B, S, 2 * DIFF_HEADS, DIFF_HEAD_DIM), positions, ROPE_THETA, DIFF_ROT_DIM)
    q = q.reshape(B, S, DIFF_HEADS, 2, DIFF_HEAD_DIM)
    k = k.reshape(B, S, DIFF_HEADS, 2, DIFF_HEAD_DIM)
    q1, q2 = q[..., 0, :], q[..., 1, :]
    k1, k2 = k[..., 0, :], k[..., 1, :]
    v = vd.reshape(B, S, DIFF_HEADS, DIFF_V_DIM)
    f32 = jnp.float32
    lam = (jnp.exp(jnp.sum(lq1.astype(f32) * lk1.astype(f32)))
           - jnp.exp(jnp.sum(lq2.astype(f32) * lk2.astype(f32))) + lambda_init)
    scale = 1.0 / math.sqrt(DIFF_HEAD_DIM)

    def block(qs):
        q1b, q2b = qs
        a = softmax_map(q1b, k1, scale) - lam * softmax_map(q2b, k2, scale)
        return jnp.einsum('bhqk,bkhd->bqhd', a.astype(v.dtype), v)

    o = merge_blocks(lax.map(block, (query_blocks(q1), query_blocks(q2))))
    o = rms_norm(o, subln) * (1.0 - lambda_init)
    return o.reshape(B, S, DIFF_OUT)


def setup_inputs(seed: int = 0) -> dict:
    key = jax.random.key(seed)
    ks = jax.random.split(key, 24)
    f32 = jnp.float32

    def w(k, shape, fan_in):
        return jax.random.normal(k, shape, f32) * fan_in ** -0.5

    def gain(k, shape):
        return 1.0 + 0.02 * jax.random.normal(k, shape, f32)

    L = DEPTH
    return {
        "x": jax.random.normal(ks[0], (BATCH, SEQ, D_MODEL), f32),
        "positions": jnp.broadcast_to(jnp.arange(SEQ, dtype=jnp.int32), (BATCH, SEQ)),
        "attn_norm": gain(ks[1], (L, D_MODEL)),
        "w_in": w(ks[2], (L, D_MODEL, IN_COLS), D_MODEL),
        "mla_q_a_norm": gain(ks[3], (L, MLA_Q_RANK)),
        "w_q_b": w(ks[4], (L, MLA_Q_RANK, MLA_HEADS * MLA_QK_DIM), MLA_Q_RANK),
        "mla_kv_a_norm": gain(ks[5], (L, MLA_KV_RANK)),
        "w_kv_b": w(ks[6], (L, MLA_KV_RANK, MLA_HEADS * (MLA_NOPE_DIM + MLA_V_DIM)), MLA_KV_RANK),
        "mla_q_norm": gain(ks[7], (L, MLA_QK_DIM)),
        "mla_k_norm": gain(ks[8], (L, MLA_QK_DIM)),
        "diff_q_norm": gain(ks[9], (L, DIFF_HEAD_DIM)),
        "diff_k_norm": gain(ks[10], (L, DIFF_HEAD_DIM)),
        "diff_lambda_q1": 0.1 * jax.random.normal(ks[11], (L, DIFF_HEAD_DIM), f32),
        "diff_lambda_k1": 0.1 * jax.random.normal(ks[12], (L, DIFF_HEAD_DIM), f32),
        "diff_lambda_q2": 0.1 * jax.random.normal(ks[13], (L, DIFF_HEAD_DIM), f32),
        "diff_lambda_k2": 0.1 * jax.random.normal(ks[14], (L, DIFF_HEAD_DIM), f32),
        "diff_subln": gain(ks[15], (L, DIFF_V_DIM)),
        "w_o": w(ks[16], (L, D_MIX, D_MODEL), D_MIX),
        "ffn_norm": gain(ks[17], (L, D_MODEL)),
        "w_gate": w(ks[18], (L, D_MODEL, D_FF), D_MODEL),
        "w_up": w(ks[19], (L, D_MODEL, D_FF), D_MODEL),
        "w_down": w(ks[20], (L, D_FF, D_MODEL), D_FF),
    }


def reference(x, positions, attn_norm, w_in, mla_q_a_norm, w_q_b, mla_kv_a_norm, w_kv_b,
              mla_q_norm, mla_k_norm, diff_q_norm, diff_k_norm, diff_lambda_q1, diff_lambda_k1,
              diff_lambda_q2, diff_lambda_k2, diff_subln, w_o, ffn_norm, w_gate, w_up, w_down):
    split_points = list(np.cumsum(COL_SIZES)[:-1])
    for l in range(DEPTH):
        lambda_init = 0.8 - 0.6 * math.exp(-0.3 * l)
        h = rms_norm(x, attn_norm[l])
        proj = h @ w_in[l]
        cq, ckv, k_pe, qd, kd, vd = jnp.split(proj, split_points, axis=-1)
        o_mla = mla_mixer(cq, ckv, k_pe, positions, mla_q_a_norm[l], w_q_b[l],
                          mla_kv_a_norm[l], w_kv_b[l], mla_q_norm[l], mla_k_norm[l])
        o_diff = diff_mixer(qd, kd, vd, positions, diff_q_norm[l], diff_k_norm[l],
                            diff_lambda_q1[l], diff_lambda_k1[l], diff_lambda_q2[l],
                            diff_lambda_k2[l], diff_subln[l], lambda_init)
        x = x + jnp.concatenate([o_mla, o_diff], axis=-1) @ w_o[l]
        h = rms_norm(x, ffn_norm[l])
        x = x + (jax.nn.silu(h @ w_gate[l]) * (h @ w_up[l])) @ w_down[l]
    return x
```

```python
import math
from collections import deque

import numpy as np
import concourse.bass as bass
import concourse.mybir as mybir
from concourse.bass_utils import run_bass_kernel_spmd

F32 = mybir.dt.float32
BF16 = mybir.dt.bfloat16
I32 = mybir.dt.int32
AF = mybir.ActivationFunctionType
ALU = mybir.AluOpType
AX = mybir.AxisListType

D = 1024
KC = 8
S = 2048
TT = 512
NT = S // TT
L_FULL = 4
NCORES = 8
B_PER_CORE = 2
FF = 2816
NFC = FF // 128
EPS = 1e-6
MLA_THETA = 10000.0
ROPE_THETA = 500000.0
WL_COLS = 416
WSLOT = 4096
NWSLOT = 3


def lambda_init(l):
    return 0.8 - 0.6 * math.exp(-0.3 * l)


class Sched:
    ENGS = ("pe", "act", "dve", "pool", "sp")

    def __init__(self):
        self.ops = {e: [] for e in self.ENGS}
        self.w = {}
        self.r = {}
        self.seen = {e: {} for e in self.ENGS}
        self.dma_val = {}

    def _need(self, eng, tok, is_raw):
        if tok is None:
            return False
        kind, src, val = tok
        if kind == "E" and src == eng:
            if eng == "pe" or eng == "sp":
                return False
        seen = self.seen[eng]
        k = (kind, src)
        if seen.get(k, -1) >= val:
            return False
        seen[k] = val
        return True

    def _deps(self, eng, reads, writes):
        waits = []
        for k in reads:
            t = self.w.get(k)
            if self._need(eng, t, True):
                waits.append(t)
        for k in writes:
            t = self.w.get(k)
            if self._need(eng, t, True):
                waits.append(t)
            for t in self.r.get(k, ()):
                if self._need(eng, t, False):
                    waits.append(t)
        return waits

    def add(self, eng, fn, reads=(), writes=()):
        waits = self._deps(eng, reads, writes)
        idx = len(self.ops[eng])
        tok = ("E", eng, idx)
        self.ops[eng].append({"fn": fn, "waits": waits, "signal": False, "dma": None})
        for k in writes:
            self.w[k] = tok
            self.r[k] = []
        for k in reads:
            self.r.setdefault(k, []).append(tok)
        return tok

    def dma(self, queue, fn, sem, reads=(), writes=()):
        waits = self._deps(queue, reads, writes)
        v = self.dma_val.get(sem, 0) + 16
        self.dma_val[sem] = v
        tok = ("D", sem, v)
        self.ops[queue].append({"fn": fn, "waits": waits, "signal": False, "dma": sem})
        for k in writes:
            self.w[k] = tok
            self.r[k] = []
        for k in reads:
            self.r.setdefault(k, []).append(tok)
        return tok

    def wait_all_dma(self, eng, sems):
        waits = [("D", s, self.dma_val[s]) for s in sems if self.dma_val.get(s, 0) > 0]
        self.ops[eng].append({"fn": None, "waits": waits, "signal": False, "dma": None})

    def barrier(self, engs=("pe", "act", "dve")):
        last = {}
        for e in engs:
            i = len(self.ops[e]) - 1
            while i >= 0 and (self.ops[e][i]["fn"] is None or self.ops[e][i]["dma"] is not None):
                i -= 1
            last[e] = i
        for e in engs:
            waits = []
            for o in engs:
                if o == e or last[o] < 0:
                    continue
                t = ("E", o, last[o])
                if self._need(e, t, True):
                    waits.append(t)
            if waits:
                self.ops[e].append({"fn": None, "waits": waits, "signal": False, "dma": None})

    def emit(self, nc):
        for e in self.ENGS:
            for op in self.ops[e]:
                for (kind, src, val) in op["waits"]:
                    if kind == "E":
                        self.ops[src][val]["signal"] = True
        sigval = {}
        for e in self.ENGS:
            c = 0
            for i, op in enumerate(self.ops[e]):
                if op["signal"]:
                    c += 1
                    sigval[(e, i)] = c
        dma_sems = sorted(self.dma_val.keys())
        import contextlib
        with contextlib.ExitStack() as st:
            esem = {e: st.enter_context(nc.semaphore("e_" + e)) for e in self.ENGS}
            dsem = {s: st.enter_context(nc.semaphore("d_" + s)) for s in dma_sems}
            block = st.enter_context(nc.Block())

            def run(eng_name):
                def body(eng):
                    for i, op in enumerate(self.ops[eng_name]):
                        for (kind, src, val) in op["waits"]:
                            if kind == "E":
                                eng.wait_ge(esem[src], sigval[(src, val)])
                            else:
                                eng.wait_ge(dsem[src], val)
                        if op["fn"] is None:
                            continue
                        ins = op["fn"](eng)
                        if op["dma"] is not None:
                            ins.then_inc(dsem[op["dma"]], 16)
                        elif op["signal"]:
                            ins.then_inc(esem[eng_name], 1)
                return body

            block.tensor(run("pe"))
            block.scalar(run("act"))
            block.vector(run("dve"))
            block.gpsimd(run("pool"))
            block.sync(run("sp"))


GC = {}


def _gc_layout(L):
    off = 0
    for name, n in (("g1", L * 8), ("g2", L * 8), ("gqa", L * 2), ("gkva", L), ("gq", L),
                    ("gk", L), ("gdq", L), ("gdk", L), ("gsub", L)):
        GC[name] = off
        off += n
    return off


def host_prepare(inp, L):
    f = lambda a: np.ascontiguousarray(np.asarray(a), dtype=np.float32)
    ng = _gc_layout(L)
    gcols = np.zeros((128, ng), np.float32)
    an, fn_ = f(inp["attn_norm"]), f(inp["ffn_norm"])
    for l in range(L):
        gcols[:, GC["g1"] + l * 8: GC["g1"] + l * 8 + 8] = an[l].reshape(8, 128).T
        gcols[:, GC["g2"] + l * 8: GC["g2"] + l * 8 + 8] = fn_[l].reshape(8, 128).T
        gcols[:, GC["gqa"] + l * 2: GC["gqa"] + l * 2 + 2] = f(inp["mla_q_a_norm"])[l].reshape(2, 128).T
        gcols[:, GC["gkva"] + l] = f(inp["mla_kv_a_norm"])[l]
        gcols[:96, GC["gq"] + l] = f(inp["mla_q_norm"])[l]
        gcols[:96, GC["gk"] + l] = f(inp["mla_k_norm"])[l]
        gcols[:64, GC["gdq"] + l] = f(inp["diff_q_norm"])[l]
        gcols[64:, GC["gdq"] + l] = f(inp["diff_q_norm"])[l]
        gcols[:64, GC["gdk"] + l] = f(inp["diff_k_norm"])[l]
        gcols[64:, GC["gdk"] + l] = f(inp["diff_k_norm"])[l]
        gcols[:, GC["gsub"] + l] = f(inp["diff_subln"])[l]
    lam = np.stack([f(inp["diff_lambda_q1"])[:L], f(inp["diff_lambda_k1"])[:L],
                    f(inp["diff_lambda_q2"])[:L], f(inp["diff_lambda_k2"])[:L]], 0)
    lamrep = np.ascontiguousarray(np.broadcast_to(lam.reshape(1, 4 * L * 64), (128, 4 * L * 64)))

    w_in = f(inp["w_in"])[:L]
    win_t = w_in.reshape(L, 8, 128, 1952).transpose(0, 2, 1, 3)
    WL = np.ascontiguousarray(win_t[:, :, :, 0:416]).reshape(L, 128, 8 * 416)
    WD = np.zeros((L, 4, 128, 8, 384), np.float32)
    for j in range(4):
        WD[:, j, :, :, 0:128] = win_t[:, :, :, 416 + 128 * j: 416 + 128 * j + 128]
        WD[:, j, :, :, 128:256] = win_t[:, :, :, 928 + 128 * j: 928 + 128 * j + 128]
        WD[:, j, :, :, 256:384] = win_t[:, :, :, 1440 + 128 * j: 1440 + 128 * j + 128]
    WD = WD.reshape(L, 4, 128, 8 * 384)
    wqb = f(inp["w_q_b"])[:L].reshape(L, 2, 128, 8, 96)
    wq = np.zeros((L, 128, 2, 8, 96), np.float32)
    wq[..., 0:32] = wqb.transpose(0, 2, 1, 3, 4)[..., 64:96]
    wq[..., 32:96] = wqb.transpose(0, 2, 1, 3, 4)[..., 0:64]
    wkvb = f(inp["w_kv_b"])[:L].reshape(L, 128, 8, 128)
    wk = np.zeros((L, 128, 8, 96), np.float32)
    wk[..., 32:96] = wkvb[..., 0:64]
    wv = wkvb[..., 64:128]
    WM = np.concatenate([wq.reshape(L, 128, 1536), wk.reshape(L, 128, 768), wv.reshape(L, 128, 512)], -1)
    WO = f(inp["w_o"])[:L].reshape(L, 8, 128, 1024)
    wg = f(inp["w_gate"])[:L].reshape(L, 8, 128, 11, 256).transpose(0, 3, 2, 1, 4)
    wu = f(inp["w_up"])[:L].reshape(L, 8, 128, 11, 256).transpose(0, 3, 2, 1, 4)
    WGU = np.ascontiguousarray(np.stack([wg, wu], 3)).reshape(L, 11, 128, 2 * 8 * 256)
    WDN = np.ascontiguousarray(
        f(inp["w_down"])[:L].reshape(L, 22, 128, 8, 128).transpose(0, 3, 2, 1, 4)).reshape(L, 8, 128, 22 * 128)

    cm = np.zeros((128, 5, 128), np.float32)
    cm[:, 0, :] = 1.0
    cm[0:64, 1, 0:64] = 1.0
    cm[64:128, 1, 64:128] = 1.0
    for i in range(16):
        cm[i + 16, 2, i] = -1.0
        cm[i, 2, i + 16] = 1.0
    for blk in (0, 64):
        for i in range(8):
            cm[blk + i + 8, 3, blk + i] = -1.0
            cm[blk + i, 3, blk + i + 8] = 1.0
    for i in range(32):
        cm[i, 4, i] = 1.0
    ident = np.eye(128, dtype=np.float32)
    ck = np.zeros((128, 4), np.float32)
    invM = np.exp(-math.log(MLA_THETA) * np.arange(16, dtype=np.float32) * (2.0 / 32)).astype(np.float32)
    invD = np.exp(-math.log(ROPE_THETA) * np.arange(8, dtype=np.float32) * (2.0 / 16)).astype(np.float32)
    for p in range(32):
        ck[p, 0] = invM[p % 16]; ck[p, 1] = 0.5 * math.pi
        ck[32 + p, 0] = invM[p % 16]; ck[32 + p, 1] = 0.0
    for p in range(16):
        ck[64 + p, 0] = invD[p % 8]; ck[64 + p, 1] = 0.5 * math.pi
        ck[96 + p, 0] = invD[p % 8]; ck[96 + p, 1] = 0.0
        ck[p, 2] = invD[p % 8]; ck[p, 3] = 0.5 * math.pi
    shared = {"gcols": gcols, "lamrep": lamrep, "WL": WL, "WD": WD, "WM": np.ascontiguousarray(WM),
              "WO": np.ascontiguousarray(WO), "WGU": WGU, "WDN": WDN,
              "cmat": cm.reshape(128, 5 * 128), "ident": ident, "ck": ck}
    return shared


def build_program(L=L_FULL, NB=B_PER_CORE, do_mla=True, do_diff=True, do_ffn=True):
    nc = bass.Bass("TRN2", target_bir_lowering=False)
    ng = _gc_layout(L)
    dt_in = lambda name, shape, dt=F32: nc.dram_tensor(name, shape, dt, kind="ExternalInput").ap()
    xin = dt_in("xin", [NB, S, D])
    posrep = dt_in("posrep", [NB, 128, S], I32)
    gcols_d = dt_in("gcols", [128, ng])
    lamrep_d = dt_in("lamrep", [128, 4 * L * 64])
    WL_d = dt_in("WL", [L, 128, 8 * 416])
    WD_d = dt_in("WD", [L, 4, 128, 8 * 384])
    WM_d = dt_in("WM", [L, 128, 2816])
    WO_d = dt_in("WO", [L, 8, 128, 1024])
    WGU_d = dt_in("WGU", [L, 11, 128, 4096])
    WDN_d = dt_in("WDN", [L, 8, 128, 2816])
    cmat_d = dt_in("cmat", [128, 5 * 128])
    ident_d = dt_in("ident", [128, 128])
    ck_d = dt_in("ck", [128, 4])
    yout = nc.dram_tensor("yout", [NB, S, D], F32, kind="ExternalOutput").ap()

    sb = nc.alloc_sbuf_tensor
    NPT, NATT = 3, 2
    xT = sb("xT", [128, KC, S], F32)
    NU = 16384 + 8192 + 12288 + NPT * 512 + NATT * 512
    U = sb("U", [128, NU], BF16)
    Wr = sb("Wr", [128, NWSLOT, WSLOT], BF16)
    TA = sb("TA", [128, S], F32)
    TB = sb("TB", [128, S], F32)
    NTF = 8
    tfb = sb("tfb", [128, NTF, TT], F32)
    o1b = sb("o1b", [128, 2, TT], F32)
    NTB = 4
    tbb = sb("tbb", [128, NTB, TT], BF16)
    cmat = sb("cmat_s", [128, 5, 128], BF16)
    ident = sb("ident_s", [128, 128], F32)
    gcols = sb("gcols_s", [128, ng], F32)
    ck = sb("ck_s", [128, 4], F32)
    lamc = sb("lamc", [128, 4 * L], F32)
    rkb = sb("rkb", [128, 2, 16], F32)
    ps = nc.alloc_psum_tensor("ps", [128, 8, TT], F32)

    def uview(off, dims):
        n = int(np.prod(dims))
        ap = U[:, off:off + n]
        if len(dims) == 2:
            return ap.rearrange("p (a b) -> p a b", a=dims[0])
        return ap

    o = 0
    hT = uview(o, (KC, S)); o += KC * S
    cqn = uview(o, (2, S)); o += 2 * S
    ckvn = U[:, o:o + S]; o += S
    kpe = U[:, o:o + S]; o += S
    slots = []
    for s_ in range(2):
        qs = U[:, o:o + S]; o += S
        ks = U[:, o:o + S]; o += S
        vs = uview(o, (16, 128)); o += S
        slots.append((qs, ks, vs))
    PT = uview(o, (NPT, TT)); o += NPT * TT
    ATT = uview(o, (NATT, TT)); o += NATT * TT
    assert o == NU
    hTh = uview(0, (KC, 2 * TT))
    actT = uview(KC * 2 * TT, (NFC, 2 * TT))
    assert KC * 2 * TT + NFC * 2 * TT <= NU

    ONES = cmat[:, 0, :]
    BONES = cmat[:, 1, :]
    RM = cmat[:, 2, :]
    RD = cmat[:, 3, :]
    EM = cmat[:, 4, :]

    sc = Sched()
    A = sc.add

    def gcol(name, idx, p0=0, p1=128):
        c = GC[name] + idx
        return gcols[p0:p1, c:c + 1]

    cnt = {"bank": 0}
    tf_free = list(range(NTF))
    tb_free = list(range(NTB))

    def tf():
        assert tf_free, "out of f32 temps"
        i = tf_free.pop(0)
        return tfb[:, i, :], ("tf", i)

    def tb():
        assert tb_free, "out of bf16 temps"
        i = tb_free.pop(0)
        return tbb[:, i, :], ("tb", i)

    def rel(*keys):
        for k in keys:
            (tf_free if k[0] == "tf" else tb_free).append(k[1])

    misc_banks = [list(range(8))]

    def bank():
        lst = misc_banks[0]
        i = lst[cnt["bank"] % len(lst)]
        cnt["bank"] += 1
        return ps[:, i, :], ("ps", i)

    def mm(out, lhsT, rhs, start, stop, reads, writes):
        A("pe", lambda e: e.matmul(out, lhsT, rhs, start=start, stop=stop), reads, writes)

    def act(out, in_, func, reads, writes, scale=1.0, bias=0.0):
        A("act", lambda e: e.activation(out=out, in_=in_, func=func, scale=scale, bias=bias), reads, writes)

    def stt(out, in0, scalar, in1, op0, op1, reads, writes, eng="dve"):
        A(eng, lambda e: e.scalar_tensor_tensor(out=out, in0=in0, scalar=scalar, in1=in1, op0=op0, op1=op1),
          reads, writes)

    def tt(out, in0, in1, op, reads, writes, eng="dve"):
        A(eng, lambda e: e.tensor_tensor(out=out, in0=in0, in1=in1, op=op), reads, writes)

    def ts(out, in0, s1, s2, op0, op1, reads, writes, eng="dve"):
        if s2 is None:
            A(eng, lambda e: e.tensor_scalar(out=out, in0=in0, scalar1=s1, scalar2=None, op0=op0), reads, writes)
        else:
            A(eng, lambda e: e.tensor_scalar(out=out, in0=in0, scalar1=s1, scalar2=s2, op0=op0, op1=op1),
              reads, writes)

    def vcopy(out, in_, reads, writes, eng="dve"):
        if eng == "act":
            act(out, in_, AF.Copy, reads, writes)
        else:
            A(eng, lambda e: e.tensor_copy(out=out, in_=in_), reads, writes)

    def recip(out, in_, reads, writes):
        act(out, in_, AF.Ln, reads, writes)
        act(out, out, AF.Exp, writes, writes, scale=-1.0)

    wstate = {"free": list(range(NWSLOT)), "pending": deque()}

    class WBlock:
        def __init__(self, src, n):
            self.src, self.n, self.slot, self.key = src, n, None, None

    def _w_issue(blk):
        slot = wstate["free"].pop(0)
        blk.slot = slot
        blk.key = ("w", slot)
        dst = Wr[:, slot, 0:blk.n]
        src = blk.src
        sc.dma("pool", lambda e: e.dma_start(out=dst, in_=src), "w%d" % slot, reads=(), writes=(blk.key,))

    def w_prefetch(src, n):
        blk = WBlock(src, n)
        if wstate["free"] and not wstate["pending"]:
            _w_issue(blk)
        else:
            wstate["pending"].append(blk)
        return blk

    def w_use(blk):
        assert blk.slot is not None, "weight block not issued (ring too small for this schedule)"
        return Wr[:, blk.slot, :], blk.key

    def w_release(blk):
        wstate["free"].append(blk.slot)
        while wstate["free"] and wstate["pending"]:
            _w_issue(wstate["pending"].popleft())

    sc.dma("sp", lambda e: e.dma_start(out=gcols[:, :], in_=gcols_d), "c0", writes=("gcols",))
    sc.dma("sp", lambda e: e.dma_start(out=ident[:, :], in_=ident_d), "c1", writes=("ident",))
    sc.dma("sp", lambda e: e.dma_start(out=ck[:, :], in_=ck_d), "c2", writes=("ck",))
    sc.dma("pool", lambda e: e.dma_start(out=cmat[:, :, :], in_=cmat_d.rearrange("p (a b) -> p a b", a=5)),
           "c3", writes=("cmat",))
    lam_v = TA[:, 0:4 * L * 64]
    sc.dma("sp", lambda e: e.dma_start(out=lam_v, in_=lamrep_d), "c4", writes=("TA",))
    lam4 = lam_v.rearrange("p (a l d) -> p a l d", a=4, l=L)
    pr = TB[:, 0:2 * L * 64].rearrange("p (a l d) -> p a l d", a=2, l=L)
    tt(pr[:, 0, :, :], lam4[:, 0, :, :], lam4[:, 1, :, :], ALU.mult, ("TA",), ("TB",))
    tt(pr[:, 1, :, :], lam4[:, 2, :, :], lam4[:, 3, :, :], ALU.mult, ("TA",), ("TB",))
    sums = lamc[:, 2 * L:4 * L]
    A("dve", lambda e: e.tensor_reduce(out=sums, in_=TB[:, 0:2 * L * 64].rearrange("p (a d) -> p a d", d=64),
                                       axis=AX.X, op=ALU.add), ("TB",), ("lamc",))
    act(sums, sums, AF.Exp, ("lamc",), ("lamc",))
    tt(lamc[:, 0:L], lamc[:, 2 * L:3 * L], lamc[:, 3 * L:4 * L], ALU.subtract, ("lamc",), ("lamc",))
    for l in range(L):
        li = lambda_init(l)
        ts(lamc[:, l:l + 1], lamc[:, l:l + 1], -1.0, -li, ALU.mult, ALU.add, ("lamc",), ("lamc",))
        ts(lamc[:, L + l:L + l + 1], gcol("gsub", l), 1.0 - li, None, ALU.mult, None, ("gcols", "lamc"), ("lamc",))

    def nlam(l):
        return lamc[:, l:l + 1]

    def gsubs(l):
        return lamc[:, L + l:L + l + 1]

    st_in = [tfb[:, 0:2, :], tfb[:, 2:4, :]]
    st_keys = [[("tf", 0), ("tf", 1)], [("tf", 2), ("tf", 3)]]

    chain_live = [0]
    maxc = [3]

    def chain(fill_A, P, nrm_n, onesmat, gname, gidx, buf, bkey, t, kind):
        tsl = slice(t * TT, (t + 1) * TT)
        while chain_live[0] >= maxc[0]:
            yield 0.7
        chain_live[0] += 1
        Ak, akey = bank()
        fill_A(Ak, akey)
        qf, qfk = tf()
        vcopy(qf[0:P, :], Ak[0:P, :], (akey,), (qfk,))
        sq, sqk = tb()
        if kind == "mla":
            tt(sq[0:P, :], qf[0:P, :], qf[0:P, :], ALU.mult, (qfk,), (sqk,), eng="pool")
        else:
            act(sq[0:P, :], qf[0:P, :], AF.Square, (qfk,), (sqk,))
        PR = 32 if kind == "mla" else P
        ts(buf[0:PR, tsl], qf[0:PR, :], gcol(gname, gidx, 0, PR), None, ALU.mult, None,
           (qfk, "gcols"), (bkey,))
        yield (4.0 if kind == "mla" else 3.0)
        Bk, bkey2 = bank()
        mm(Bk[0:P, :], onesmat[0:P, 0:P], sq[0:P, :], True, True, (sqk, "cmat"), (bkey2,))
        Ck, ckey = bank()
        if kind == "mla":
            mm(Ck[0:32, :], RM[0:32, 0:32], buf[0:32, tsl], True, True, (bkey, "cmat"), (ckey,))
            segs = ((0, 32, TA, 0, "TA", TA, 32),)
        else:
            mm(Ck[:, :], RD[:, :], buf[:, tsl], True, True, (bkey, "cmat"), (ckey,))
            segs = ((0, 16, TB, 0, "TB", TA, 96), (64, 16, TA, 64, "TA", TA, 96))
        t1, t1k = tf()
        for (p0, n, cosT, c0, ckn, sinT, s0) in segs:
            tt(t1[p0:p0 + n, :], Ck[p0:p0 + n, :], sinT[s0:s0 + n, tsl], ALU.mult, (ckey, "TA"), (t1k,))
        rt, rtk = tf()
        act(rt[0:P, :], Bk[0:P, :], AF.Ln, (bkey2,), (rtk,), scale=1.0 / nrm_n, bias=EPS)
        act(rt[0:P, :], rt[0:P, :], AF.Exp, (rtk,), (rtk,), scale=-0.5)
        stt(buf[0:P, tsl], qf[0:P, :], gcol(gname, gidx, 0, P), rt[0:P, :], ALU.mult, ALU.mult,
            (qfk, rtk, "gcols"), (bkey,))
        t0, t0k = tf()
        for (p0, n, cosT, c0, ckn, sinT, s0) in segs:
            tt(t1[p0:p0 + n, :], t1[p0:p0 + n, :], rt[p0:p0 + n, :], ALU.mult, (t1k, rtk), (t1k,), eng="pool")
            tt(t0[p0:p0 + n, :], buf[p0:p0 + n, tsl], cosT[c0:c0 + n, tsl], ALU.mult, (bkey, ckn), (t0k,), eng="pool")
            tt(buf[p0:p0 + n, tsl], t0[p0:p0 + n, :], t1[p0:p0 + n, :], ALU.add, (t0k, t1k), (bkey,), eng="pool")
        rel(qfk, sqk, rtk, t1k, t0k)
        chain_live[0] -= 1
        yield 0.0

    def norm_tile(l, gname, t, dst, dkey_fn, dst_t):
        tsl = slice(t * TT, (t + 1) * TT)
        dsl = slice(dst_t * TT, (dst_t + 1) * TT)
        Bk, bkey = bank()
        for kc in range(KC):
            sq, sqk = tb()
            act(sq[:, :], xT[:, kc, tsl], AF.Square, (("xT", kc, t),), (sqk,))
            mm(Bk[:, :], ONES, sq[:, :], kc == 0, kc == KC - 1, (sqk, "cmat"), (bkey,))
            rel(sqk)
        rt, rtk = tf()
        act(rt[:, :], Bk[:, :], AF.Ln, (bkey,), (rtk,), scale=1.0 / D, bias=EPS)
        act(rt[:, :], rt[:, :], AF.Exp, (rtk,), (rtk,), scale=-0.5)
        for kc in range(KC):
            stt(dst[:, kc, dsl], xT[:, kc, tsl], gcol(gname, l * 8 + kc), rt[:, :], ALU.mult, ALU.mult,
                (("xT", kc, t), rtk, "gcols"), (dkey_fn(kc, dst_t),))
        rel(rtk)

    hkey = lambda kc, t: ("hT", kc, t)

    def latents(l):
        blk = w_prefetch(WL_d[l], 8 * 416)
        wl, wkey = w_use(blk)
        wl3 = wl[:, 0:8 * 416].rearrange("p (k c) -> p k c", k=8)
        for t in range(NT):
            tsl = slice(t * TT, (t + 1) * TT)
            raws = []
            for c in range(3):
                Ak, akey = bank()
                for kc in range(KC):
                    mm(Ak[:, :], wl3[:, kc, c * 128:(c + 1) * 128], hT[:, kc, tsl], kc == 0, kc == KC - 1,
                       (wkey, hkey(kc, t)), (akey,))
                raws.append((Ak, akey))
            Ak, akey = bank()
            for kc in range(KC):
                mm(Ak[0:32, :], wl3[:, kc, 384:416], hT[:, kc, tsl], kc == 0, kc == KC - 1,
                   (wkey, hkey(kc, t)), (akey,))
            act(kpe[0:32, tsl], Ak[0:32, :], AF.Copy, (akey,), (("kpe", t),))
            B0, b0k = bank()
            B1, b1k = bank()
            for c in range(3):
                sq, sqk = tb()
                act(sq[:, :], raws[c][0][:, :], AF.Square, (raws[c][1],), (sqk,))
                if c < 2:
                    mm(B0[:, :], ONES, sq[:, :], c == 0, c == 1, (sqk, "cmat"), (b0k,))
                else:
                    mm(B1[:, :], ONES, sq[:, :], True, True, (sqk, "cmat"), (b1k,))
                rel(sqk)
            r0, r0k = tf()
            act(r0[:, :], B0[:, :], AF.Ln, (b0k,), (r0k,), scale=1.0 / 256, bias=EPS)
            act(r0[:, :], r0[:, :], AF.Exp, (r0k,), (r0k,), scale=-0.5)
            r1, r1k = tf()
            act(r1[:, :], B1[:, :], AF.Ln, (b1k,), (r1k,), scale=1.0 / 128, bias=EPS)
            act(r1[:, :], r1[:, :], AF.Exp, (r1k,), (r1k,), scale=-0.5)
            for c in range(2):
                stt(cqn[:, c, tsl], raws[c][0][:, :], gcol("gqa", l * 2 + c), r0[:, :], ALU.mult, ALU.mult,
                    (raws[c][1], r0k, "gcols"), (("cqn", c, t),))
            stt(ckvn[:, tsl], raws[2][0][:, :], gcol("gkva", l), r1[:, :], ALU.mult, ALU.mult,
                (raws[2][1], r1k, "gcols"), (("ckvn", t),))
            rel(r0k, r1k)
            for _ in range(8):
                bg_step()
                bg_tick(1.0)
        w_release(blk)

    bg = {"tasks": [], "now": 0.0, "seq": 0}

    def bg_add(gen, prio, delay=0.0):
        bg["tasks"].append([prio, bg["seq"], bg["now"] + delay, gen])
        bg["seq"] += 1

    def bg_step():
        while True:
            cands = [t for t in bg["tasks"] if t[2] <= bg["now"] + 1e-9]
            if not cands:
                return False
            t = min(cands, key=lambda t_: (t_[0], t_[1]))
            try:
                d = next(t[3])
                t[2] = bg["now"] + (d or 0.0)
                return True
            except StopIteration:
                bg["tasks"].remove(t)

    def bg_tick(dt):
        bg["now"] += dt

    def bg_finish(max_prio_left):
        while any(t[0] >= max_prio_left for t in bg["tasks"]):
            if not bg_step():
                bg_tick(0.5)

    def counted(g, state, on_done):
        for d in g:
            yield d
        state[0] -= 1
        if state[0] == 0 and on_done is not None:
            on_done()

    def prep_mla_tasks(l, h, slot, wm, wmkey):
        qs, ks, vs = slots[slot]
        wq = wm[:, 0:1536].rearrange("p (k h d) -> p k h d", k=2, h=8)
        wk = wm[:, 1536:2304].rearrange("p (h d) -> p h d", h=8)
        wv = wm[:, 2304:2816].rearrange("p (h d) -> p h d", h=8)
        voff = 0 if h % 2 == 0 else 64
        ooff = 64 - voff
        A("pool", lambda e: e.memset(vs[:, :, ooff:ooff + 64], 1.0), (), (("vs", slot),))

        def fq(t):
            def f(Ak, akey):
                tsl = slice(t * TT, (t + 1) * TT)
                for kc in range(2):
                    mm(Ak[0:96, :], wq[:, kc, h, :], cqn[:, kc, tsl], kc == 0, kc == 1,
                       (wmkey, ("cqn", kc, t)), (akey,))
            return f

        def fk(t):
            def f(Ak, akey):
                tsl = slice(t * TT, (t + 1) * TT)
                mm(Ak[0:96, :], wk[:, h, :], ckvn[:, tsl], True, False, (wmkey, ("ckvn", t)), (akey,))
                mm(Ak[0:96, :], EM[0:32, 0:96], kpe[0:32, tsl], False, True, ("cmat", ("kpe", t)), (akey,))
            return f

        def kchain(t):
            tsl = slice(t * TT, (t + 1) * TT)
            bkey = ("ks", slot, t)
            while chain_live[0] >= maxc[0]:
                yield 0.7
            chain_live[0] += 1
            Ak, akey = bank()
            fk(t)(Ak, akey)
            qf, qfk = tf()
            vcopy(qf[0:96, :], Ak[0:96, :], (akey,), (qfk,))
            sq, sqk = tb()
            tt(sq[0:96, :], qf[0:96, :], qf[0:96, :], ALU.mult, (qfk,), (sqk,), eng="pool")
            ts(ks[0:96, tsl], qf[0:96, :], gcol("gk", l, 0, 96), None, ALU.mult, None, (qfk, "gcols"), (bkey,))
            yield 4.0
            Ck, ckey = bank()
            mm(Ck[0:32, :], RM[0:32, 0:32], ks[0:32, tsl], True, True, (bkey, "cmat"), (ckey,))
            Rk, rkey = bank()
            for j in range(4):
                mm(Rk[:, j:j + 1], sq[0:96, j * 128:(j + 1) * 128], ONES[0:96, 0:1], True, True,
                   (sqk, "cmat"), (rkey,))
            rks = rkb[:, slot, 4 * t:4 * t + 4]
            act(rks, Rk[:, 0:4], AF.Ln, (rkey,), (("rk", slot, t),), scale=1.0 / 96.0, bias=EPS)
            act(rks, rks, AF.Exp, (("rk", slot, t),), (("rk", slot, t),), scale=-0.5,
                bias=math.log(1.0 / math.sqrt(96.0)))
            t1, t1k = tf()
            tt(t1[0:32, :], Ck[0:32, :], TA[32:64, tsl], ALU.mult, (ckey, "TA"), (t1k,))
            t0, t0k = tf()
            tt(t0[0:32, :], ks[0:32, tsl], TA[0:32, tsl], ALU.mult, (bkey, "TA"), (t0k,), eng="pool")
            tt(ks[0:32, tsl], t0[0:32, :], t1[0:32, :], ALU.add, (t0k, t1k), (bkey,), eng="pool")
            rel(qfk, sqk, t1k, t0k)
            chain_live[0] -= 1
            yield 0.0

        def vgen(t):
            Vk, vkey = bank()
            for j in range(4):
                mm(Vk[:, j * 64:(j + 1) * 64], ckvn[:, t * TT + j * 128:t * TT + (j + 1) * 128], wv[:, h, :],
                   True, True, (wmkey, ("ckvn", t)), (vkey,))
            vcopy(vs[:, 4 * t:4 * t + 4, voff:voff + 64], Vk[:, 0:256].rearrange("p (a b) -> p a b", a=4),
                  (vkey,), (("vs", slot),))
            yield 0.0

        for t in range(NT):
            bg_add(chain(fq(t), 96, 96.0, ONES, "gq", l, qs, ("qs", slot, t), t, "mla"), 1)
            bg_add(kchain(t), 1)
            bg_add(vgen(t), 1)

    def prep_diff_tasks(l, j, slot):
        qs, ks, vs = slots[slot]
        blk = w_prefetch(WD_d[l, j], 8 * 384)
        state = [3 * NT]

        def wd3():
            wd, wkey = w_use(blk)
            return wd[:, 0:8 * 384].rearrange("p (k c) -> p k c", k=8), wkey

        def fqk(which, t):
            def f(Ak, akey):
                w3, wkey = wd3()
                tsl = slice(t * TT, (t + 1) * TT)
                for kc in range(KC):
                    mm(Ak[:, :], w3[:, kc, which * 128:(which + 1) * 128], hT[:, kc, tsl], kc == 0, kc == KC - 1,
                       (wkey, hkey(kc, t)), (akey,))
            return f

        def vgen(t):
            w3, wkey = wd3()
            Vk, vkey = bank()
            for j4 in range(4):
                for kc in range(KC):
                    mm(Vk[:, j4 * 128:(j4 + 1) * 128], hT[:, kc, t * TT + j4 * 128:t * TT + (j4 + 1) * 128],
                       w3[:, kc, 256:384], kc == 0, kc == KC - 1, (wkey, hkey(kc, t)), (vkey,))
            vcopy(vs[:, 4 * t:4 * t + 4, :], Vk[:, :].rearrange("p (a b) -> p a b", a=4), (vkey,), (("vs", slot),))
            yield 0.0

        rel_blk = lambda: w_release(blk)
        for t in range(NT):
            bg_add(counted(chain(fqk(0, t), 128, 64.0, BONES, "gdq", l, qs, ("qs", slot, t), t, "diff"),
                           state, rel_blk), 1)
            bg_add(counted(chain(fqk(1, t), 128, 64.0, BONES, "gdk", l, ks, ("ks", slot, t), t, "diff"),
                           state, rel_blk), 1)
            bg_add(counted(vgen(t), state, rel_blk), 1)

    att_cnt = {"pt": 0, "att": 0, "s": 0}
    busy = {}

    def att_alloc():
        ai = att_cnt["att"] % NATT
        att_cnt["att"] += 1
        assert not busy.get(("att", ai)), "att buffer still busy"
        busy[("att", ai)] = True
        return ai

    def wo_task(l, wo, wokey, krows, att_ap, attkey, qt, release_blk=None):
        p0, p1 = krows
        for m in range(KC):
            Yk, ykey = bank()
            mm(Yk[:, :], wo[p0:p1, m * 128:(m + 1) * 128], att_ap[p0:p1, :], True, True, (wokey, attkey), (ykey,))
            tsl = slice(qt * TT, (qt + 1) * TT)
            tt(xT[:, m, tsl], Yk[:, :], xT[:, m, tsl], ALU.add, (ykey, ("xT", m, qt)), (("xT", m, qt),))
            yield 0.0
        busy[attkey] = False
        if release_blk is not None:
            w_release(release_blk)

    ATTP = o1b[:, :, :].rearrange("p a b -> p (a b)").bitcast(BF16).rearrange("p (a b) -> p a b", a=4)

    def attention_head(kind, l, h, slot, scale, obanks, wo, wokey, release_blk):
        qs, ks, vs = slots[slot]
        nmap = 1 if kind == "mla" else 2
        Kr = 96 if kind == "mla" else 64
        dt_iter = 0.64 if kind == "mla" else 0.86
        iters = []
        for qt in range(NT):
            for mp in range(nmap):
                for kt in range(16):
                    iters.append((qt, mp, kt))
        LOOK = 2
        pend = deque()
        SB = (0, 1, 2)
        for i in range(len(iters) + LOOK):
            if i < len(iters):
                qt, mp, kt = iters[i]
                sbk = SB[att_cnt["s"] % 3]
                att_cnt["s"] += 1
                pti = att_cnt["pt"] % NPT
                att_cnt["pt"] += 1
                p0 = mp * 64 if kind == "diff" else 0
                qsl = slice(qt * TT, (qt + 1) * TT)
                mm(ps[:, sbk, :], ks[p0:p0 + Kr, kt * 128:(kt + 1) * 128], qs[p0:p0 + Kr, qsl], True, True,
                   (("ks", slot, kt // 4), ("qs", slot, qt)), (("ps", sbk),))
                if kind == "mla":
                    act(PT[:, pti, :], ps[:, sbk, :], AF.Exp, (("ps", sbk), ("rk", slot, kt // 4)), (("pt", pti),),
                        scale=rkb[:, slot, kt:kt + 1])
                else:
                    act(PT[:, pti, :], ps[:, sbk, :], AF.Exp, (("ps", sbk),), (("pt", pti),), scale=scale)
                pend.append((qt, mp, kt, pti))
            if i >= LOOK:
                qt, mp, kt, pti = pend.popleft()
                if kind == "mla":
                    ob = obanks[qt % 2]
                    mm(ps[:, ob, :], vs[:, kt, :], PT[:, pti, :], kt == 0, kt == 15,
                       (("vs", slot), ("pt", pti)), (("ps", ob),))
                    if kt == 15:
                        orow = 0 if h % 2 == 0 else 64
                        srow = 64 - orow
                        rr, rrk = tf()
                        recip(rr[srow:srow + 64, :], ps[srow:srow + 64, ob, :], (("ps", ob),), (rrk,))
                        tt(ATTP[orow:orow + 64, qt, :], ps[orow:orow + 64, ob, :], rr[srow:srow + 64, :], ALU.mult,
                           (("ps", ob), rrk), (("attp", qt),))
                        rel(rrk)
                        last = (qt == NT - 1)
                        if h % 2 == 1:
                            bg_add(wo_task(l, wo, wokey, (0, 128), ATTP[:, qt, :], ("attp", qt), qt,
                                           release_blk if last else None), 0, delay=3.0)
                else:
                    ob, zb = obanks
                    mm(ps[:, ob, :], vs[:, kt, :], PT[:, pti, :], kt == 0, kt == 15,
                       (("vs", slot), ("pt", pti)), (("ps", ob),))
                    mm(ps[:, zb, :], ONES, PT[:, pti, :], kt == 0, kt == 15, (("pt", pti), "cmat"), (("ps", zb),))
                    if kt == 15:
                        oc, ock = tf()
                        vcopy(oc[:, :], ps[:, ob, :], (("ps", ob),), (ock,))
                        rr, rrk = tf()
                        recip(rr[:, :], ps[:, zb, :], (("ps", zb),), (rrk,))
                        dd, ddk = o1b[:, qt % 2, :], ("o1", qt % 2)
                        if mp == 0:
                            assert not busy.get(ddk), "o1 buffer still busy"
                            busy[ddk] = True
                            tt(dd, oc[:, :], rr[:, :], ALU.mult, (ock, rrk), (ddk,))
                            rel(rrk, ock)
                        else:
                            tt(oc[:, :], oc[:, :], rr[:, :], ALU.mult, (ock, rrk), (ock,))
                            stt(dd, oc[:, :], nlam(l), dd, ALU.mult, ALU.add, (ock, ddk, "lamc"), (ddk,))
                            rel(rrk, ock)
                            sq, sqk = tb()
                            act(sq[:, :], dd, AF.Square, (ddk,), (sqk,))
                            last = (qt == NT - 1)
                            bg_add(diff_post(l, dd, ddk, sq, sqk, qt, wo, wokey, release_blk if last else None), 0,
                                   delay=4.5)
            bg_step()
            bg_tick(dt_iter)

    def diff_post(l, dd, ddk, sq, sqk, qt, wo, wokey, release_blk):
        Bk, bkey = bank()
        mm(Bk[:, :], ONES, sq[:, :], True, True, (sqk, "cmat"), (bkey,))
        rt, rtk = tf()
        act(rt[:, :], Bk[:, :], AF.Ln, (bkey,), (rtk,), scale=1.0 / 128, bias=EPS)
        act(rt[:, :], rt[:, :], AF.Exp, (rtk,), (rtk,), scale=-0.5)
        ai = att_alloc()
        stt(ATT[:, ai, :], dd, gsubs(l), rt[:, :], ALU.mult, ALU.mult, (ddk, rtk, "lamc"), (("att", ai),))
        rel(rtk, sqk)
        busy[ddk] = False
        yield 0.0
        for d in wo_task(l, wo, wokey, (0, 128), ATT[:, ai, :], ("att", ai), qt, release_blk):
            yield d

    def attention_layer(l, pre_registered_diff0):
        wmb = None
        wm = wmkey = None
        if do_diff:
            misc_banks[0] = [5, 6, 7]
            if not pre_registered_diff0:
                prep_diff_tasks(l, 0, 0)
            bg_finish(1)
            for j in range(4):
                blk = w_prefetch(WO_d[l, 4 + j], 1024)
                if j + 1 < 4:
                    prep_diff_tasks(l, j + 1, (j + 1) % 2)
                elif do_mla:
                    wmb = w_prefetch(WM_d[l], 2816)
                    wm, wmkey = w_use(wmb)
                    prep_mla_tasks(l, 0, 0, wm, wmkey)
                wo, wokey = w_use(blk)
                attention_head("diff", l, j, j % 2, 1.0 / math.sqrt(64.0), (3, 4), wo, wokey, blk)
                bg_finish(1)
        if do_mla:
            bg_finish(0)
            w_ = [t for k in (("o1", 0), ("o1", 1)) for t in ([sc.w.get(k)] + list(sc.r.get(k, []))) if t is not None]
            sc.ops["dve"].append({"fn": None, "waits": [t for t in w_ if sc._need("dve", t, True)],
                                  "signal": False, "dma": None})
            misc_banks[0] = [4, 6, 7]
            if wmb is None:
                wmb = w_prefetch(WM_d[l], 2816)
                wm, wmkey = w_use(wmb)
                prep_mla_tasks(l, 0, 0, wm, wmkey)
                bg_finish(1)
            blk = None
            for h in range(8):
                if h % 2 == 0:
                    blk = w_prefetch(WO_d[l, h // 2], 1024)
                if h + 1 < 8:
                    prep_mla_tasks(l, h + 1, (h + 1) % 2, wm, wmkey)
                wo, wokey = w_use(blk)
                attention_head("mla", l, h, h % 2, 1.0 / math.sqrt(96.0), (3, 5), wo, wokey,
                               blk if h % 2 == 1 else None)
                bg_finish(1)
        bg_finish(0)
        if wmb is not None:
            w_release(wmb)
        misc_banks[0] = list(range(8))

    def ffn_layer(l):
        for hf in range(2):
            misc_banks[0] = [6, 7]
            for tt_ in range(2):
                norm_tile(l, "g2", 2 * hf + tt_, hTh, lambda kc, t: ("hTh", kc, t), tt_)
            blks = deque()
            seq = [("gu", cp) for cp in range(11)] + [("dn", m) for m in range(8)]

            def pf(i):
                kind, idx = seq[i]
                if kind == "gu":
                    return w_prefetch(WGU_d[l, idx], 4096)
                return w_prefetch(WDN_d[l, idx], 2816)
            nxt = 0
            while nxt < min(2, len(seq)):
                blks.append(pf(nxt)); nxt += 1
            gub = 0
            for i, (kind, idx) in enumerate(seq):
                blk = blks.popleft()
                w, wkey = w_use(blk)
                if kind == "gu":
                    w4 = w[:, 0:4096].rearrange("p (a k c) -> p a k c", a=2, k=8)
                    for cc in range(2):
                        c = 2 * idx + cc
                        for tt_ in range(2):
                            gb = (gub % 2) * 2
                            gub += 1
                            dsl = slice(tt_ * TT, (tt_ + 1) * TT)
                            for a_ in range(2):
                                for kc in range(KC):
                                    mm(ps[:, gb + a_, :], w4[:, a_, kc, cc * 128:(cc + 1) * 128], hTh[:, kc, dsl],
                                       kc == 0, kc == KC - 1, (wkey, ("hTh", kc, tt_)), (("ps", gb + a_),))
                            sg, sgk = tf()
                            act(sg[:, :], ps[:, gb, :], AF.Silu, (("ps", gb),), (sgk,))
                            tt(actT[:, c, dsl], ps[:, gb + 1, :], sg[:, :], ALU.mult, (sgk, ("ps", gb + 1)),
                               (("actT", c, tt_),))
                            rel(sgk)
                else:
                    w3 = w[:, 0:2816].rearrange("p (c m) -> p c m", c=22)
                    for tt_ in range(2):
                        yb = 4 + (idx * 2 + tt_) % 2
                        dsl = slice(tt_ * TT, (tt_ + 1) * TT)
                        for c in range(NFC):
                            mm(ps[:, yb, :], w3[:, c, :], actT[:, c, dsl], c == 0, c == NFC - 1,
                               (wkey, ("actT", c, tt_)), (("ps", yb),))
                        t = 2 * hf + tt_
                        tsl = slice(t * TT, (t + 1) * TT)
                        tt(xT[:, idx, tsl], ps[:, yb, :], xT[:, idx, tsl], ALU.add,
                           (("ps", yb), ("xT", idx, t)), (("xT", idx, t),))
                w_release(blk)
                if nxt < len(seq):
                    blks.append(pf(nxt)); nxt += 1
        misc_banks[0] = list(range(8))

    def load_batch(b):
        scrF = U[:, 0:2 * S].bitcast(F32)
        scrI = U[:, 0:2 * S].bitcast(I32)
        C1 = 6.28125
        C2 = 2.0 * math.pi - C1
        sc.dma("sp", lambda e: e.dma_start(out=TB[:, :].bitcast(I32), in_=posrep[b]), "c5", writes=("TB",))
        vcopy(TA[:, :], TB[:, :].bitcast(I32), ("TB",), ("TA",))
        ts(TB[:, :], TA[:, :], ck[:, 2:3], ck[:, 3:4], ALU.mult, ALU.add, ("TA", "ck"), ("TB",))
        ts(TA[:, :], TA[:, :], ck[:, 0:1], ck[:, 1:2], ALU.mult, ALU.add, ("TA", "ck"), ("TA",))
        for (T_, k_) in ((TA, "TA"), (TB, "TB")):
            ts(scrF, T_[:, :], 1.0 / (2.0 * math.pi), None, ALU.mult, None, (k_,), ("Uscr",))
            vcopy(scrI, scrF, ("Uscr",), ("Uscr",))
            vcopy(scrF, scrI, ("Uscr",), ("Uscr",))
            stt(T_[:, :], scrF, -C1, T_[:, :], ALU.mult, ALU.add, ("Uscr", k_), (k_,))
            stt(T_[:, :], scrF, -C2, T_[:, :], ALU.mult, ALU.add, ("Uscr", k_), (k_,))
            ts(scrF, T_[:, :], math.pi, 2.0 * math.pi, ALU.is_gt, ALU.mult, (k_,), ("Uscr",))
            tt(T_[:, :], T_[:, :], scrF, ALU.subtract, (k_, "Uscr"), (k_,))
            ts(T_[:, :], T_[:, :], -math.pi, math.pi, ALU.max, ALU.min, (k_,), (k_,))
            act(T_[:, :], T_[:, :], AF.Sin, (k_,), (k_,))
        for i in range(S // 128):
            si = i % 2
            stg = st_in[si]
            sc.dma("sp", lambda e, stg=stg, i=i: e.dma_start(
                out=stg, in_=xin[b, i * 128:(i + 1) * 128, :].rearrange("p (a c) -> p a c", a=2)),
                "st%d" % si, writes=tuple(st_keys[si]))
            stf = [tfb[:, 2 * si, :], tfb[:, 2 * si + 1, :]]
            for half in range(2):
                Bk, bkey = bank()
                for q4 in range(4):
                    kc = half * 4 + q4
                    src = stf[kc // 4][:, (kc % 4) * 128:(kc % 4 + 1) * 128]
                    A("pe", lambda e, o_=Bk[:, q4 * 128:(q4 + 1) * 128], s_=src: e.transpose(o_, s_, ident[:, :]),
                      (st_keys[si][kc // 4], "ident"), (bkey,))
                t = i // 4
                vcopy(xT[:, half * 4:half * 4 + 4, i * 128:(i + 1) * 128],
                      Bk[:, :].rearrange("p (a c) -> p a c", a=4), (bkey,),
                      tuple(("xT", half * 4 + q, t) for q in range(4)), eng=("dve" if half == 0 else "act"))

    def store_batch(b):
        for i in range(S // 128):
            si = i % 2
            stf = [tfb[:, 2 * si, :], tfb[:, 2 * si + 1, :]]
            t = i // 4
            for half in range(2):
                Bk, bkey = bank()
                for q4 in range(4):
                    kc = half * 4 + q4
                    A("pe", lambda e, o_=Bk[:, q4 * 128:(q4 + 1) * 128], s_=xT[:, kc, i * 128:(i + 1) * 128]:
                      e.transpose(o_, s_, ident[:, :]), (("xT", kc, t), "ident"), (bkey,))
                vcopy(stf[half][:, :], Bk[:, :], (bkey,), (st_keys[si][half],), eng=("dve" if half == 0 else "act"))
            stg = st_in[si]
            sc.dma("sp", lambda e, stg=stg, i=i: e.dma_start(
                out=yout[b, i * 128:(i + 1) * 128, :].rearrange("p (a c) -> p a c", a=2), in_=stg),
                "so%d" % si, reads=tuple(st_keys[si]))

    for b in range(NB):
        sc.barrier()
        load_batch(b)
        for l in range(L):
            sc.barrier()
            misc_banks[0] = list(range(8))
            for t in range(NT):
                norm_tile(l, "g1", t, hT, hkey, t)
            pre = False
            if do_diff:
                prep_diff_tasks(l, 0, 0)
                pre = True
            if do_mla:
                latents(l)
            attention_layer(l, pre)
            sc.barrier()
            if do_ffn:
                ffn_layer(l)
        sc.barrier()
        store_batch(b)
    sc.wait_all_dma("sp", ["so0", "so1"])
    sc.emit(nc)
    return nc, sc


def make_in_maps(inp, L, ncores, nb):
    shared = host_prepare(inp, L)
    x = np.ascontiguousarray(np.asarray(inp["x"]), dtype=np.float32)
    pos = np.ascontiguousarray(np.asarray(inp["positions"]), dtype=np.int32)
    in_maps = []
    for c in range(ncores):
        m = dict(shared)
        m["xin"] = np.ascontiguousarray(x[c * nb:(c + 1) * nb])
        m["posrep"] = np.ascontiguousarray(np.broadcast_to(pos[c * nb:(c + 1) * nb, None, :], (nb, 128, S)))
        in_maps.append(m)
    return in_maps


_CACHE = {}


def kernel(**inputs):
    key = "full"
    if key not in _CACHE:
        _CACHE[key] = build_program(L_FULL, B_PER_CORE)[0]
    nc = _CACHE[key]
    in_maps = make_in_maps(inputs, L_FULL, NCORES, B_PER_CORE)
    res = run_bass_kernel_spmd(nc, in_maps, core_ids=list(range(NCORES)))
    out = np.concatenate([np.asarray(r["yout"]) for r in res.results], axis=0)
    return out.astype(np.float32, copy=False)
```

```python
import math
from collections import deque

import numpy as np
import concourse.bass as bass
import concourse.mybir as mybir
from concourse.bass_utils import run_bass_kernel_spmd

F32 = mybir.dt.float32
BF16 = mybir.dt.bfloat16
I32 = mybir.dt.int32
AF = mybir.ActivationFunctionType
ALU = mybir.AluOpType
AX = mybir.AxisListType

D = 1024
KC = 8
S = 2048
TT = 512
NT = S // TT
L_FULL = 4
NCORES = 8
B_PER_CORE = 2
FF = 2816
NFC = FF // 128
EPS = 1e-6
MLA_THETA = 10000.0
ROPE_THETA = 500000.0
WL_COLS = 416
WSLOT = 4096
NWSLOT = 3


def lambda_init(l):
    return 0.8 - 0.6 * math.exp(-0.3 * l)


class Sched:
    ENGS = ("pe", "act", "dve", "pool", "sp")

    def __init__(self):
        self.ops = {e: [] for e in self.ENGS}
        self.w = {}
        self.r = {}
        self.seen = {e: {} for e in self.ENGS}
        self.dma_val = {}

    def _need(self, eng, tok, is_raw):
        if tok is None:
            return False
        kind, src, val = tok
        if kind == "E" and src == eng:
            if eng == "pe" or eng == "sp":
                return False
        seen = self.seen[eng]
        k = (kind, src)
        if seen.get(k, -1) >= val:
            return False
        seen[k] = val
        return True

    def _deps(self, eng, reads, writes):
        waits = []
        for k in reads:
            t = self.w.get(k)
            if self._need(eng, t, True):
                waits.append(t)
        for k in writes:
            t = self.w.get(k)
            if self._need(eng, t, True):
                waits.append(t)
            for t in self.r.get(k, ()):
                if self._need(eng, t, False):
                    waits.append(t)
        return waits

    def add(self, eng, fn, reads=(), writes=()):
        waits = self._deps(eng, reads, writes)
        idx = len(self.ops[eng])
        tok = ("E", eng, idx)
        self.ops[eng].append({"fn": fn, "waits": waits, "signal": False, "dma": None})
        for k in writes:
            self.w[k] = tok
            self.r[k] = []
        for k in reads:
            self.r.setdefault(k, []).append(tok)
        return tok

    def dma(self, queue, fn, sem, reads=(), writes=()):
        waits = self._deps(queue, reads, writes)
        v = self.dma_val.get(sem, 0) + 16
        self.dma_val[sem] = v
        tok = ("D", sem, v)
        self.ops[queue].append({"fn": fn, "waits": waits, "signal": False, "dma": sem})
        for k in writes:
            self.w[k] = tok
            self.r[k] = []
        for k in reads:
            self.r.setdefault(k, []).append(tok)
        return tok

    def wait_all_dma(self, eng, sems):
        waits = [("D", s, self.dma_val[s]) for s in sems if self.dma_val.get(s, 0) > 0]
        self.ops[eng].append({"fn": None, "waits": waits, "signal": False, "dma": None})

    def barrier(self, engs=("pe", "act", "dve")):
        last = {}
        for e in engs:
            i = len(self.ops[e]) - 1
            while i >= 0 and (self.ops[e][i]["fn"] is None or self.ops[e][i]["dma"] is not None):
                i -= 1
            last[e] = i
        for e in engs:
            waits = []
            for o in engs:
                if o == e or last[o] < 0:
                    continue
                t = ("E", o, last[o])
                if self._need(e, t, True):
                    waits.append(t)
            if waits:
                self.ops[e].append({"fn": None, "waits": waits, "signal": False, "dma": None})

    def emit(self, nc):
        for e in self.ENGS:
            for op in self.ops[e]:
                for (kind, src, val) in op["waits"]:
                    if kind == "E":
                        self.ops[src][val]["signal"] = True
        sigval = {}
        for e in self.ENGS:
            c = 0
            for i, op in enumerate(self.ops[e]):
                if op["signal"]:
                    c += 1
                    sigval[(e, i)] = c
        dma_sems = sorted(self.dma_val.keys())
        import contextlib
        with contextlib.ExitStack() as st:
            esem = {e: st.enter_context(nc.semaphore("e_" + e)) for e in self.ENGS}
            dsem = {s: st.enter_context(nc.semaphore("d_" + s)) for s in dma_sems}
            block = st.enter_context(nc.Block())

            def run(eng_name):
                def body(eng):
                    for i, op in enumerate(self.ops[eng_name]):
                        for (kind, src, val) in op["waits"]:
                            if kind == "E":
                                eng.wait_ge(esem[src], sigval[(src, val)])
                            else:
                                eng.wait_ge(dsem[src], val)
                        if op["fn"] is None:
                            continue
                        ins = op["fn"](eng)
                        if op["dma"] is not None:
                            ins.then_inc(dsem[op["dma"]], 16)
                        elif op["signal"]:
                            ins.then_inc(esem[eng_name], 1)
                return body

            block.tensor(run("pe"))
            block.scalar(run("act"))
            block.vector(run("dve"))
            block.gpsimd(run("pool"))
            block.sync(run("sp"))


GC = {}


def _gc_layout(L):
    off = 0
    for name, n in (("g1", L * 8), ("g2", L * 8), ("gqa", L * 2), ("gkva", L), ("gq", L),
                    ("gk", L), ("gdq", L), ("gdk", L), ("gsub", L)):
        GC[name] = off
        off += n
    return off


def host_prepare(inp, L):
    f = lambda a: np.ascontiguousarray(np.asarray(a), dtype=np.float32)
    ng = _gc_layout(L)
    gcols = np.zeros((128, ng), np.float32)
    an, fn_ = f(inp["attn_norm"]), f(inp["ffn_norm"])
    for l in range(L):
        gcols[:, GC["g1"] + l * 8: GC["g1"] + l * 8 + 8] = an[l].reshape(8, 128).T
        gcols[:, GC["g2"] + l * 8: GC["g2"] + l * 8 + 8] = fn_[l].reshape(8, 128).T
        gcols[:, GC["gqa"] + l * 2: GC["gqa"] + l * 2 + 2] = f(inp["mla_q_a_norm"])[l].reshape(2, 128).T
        gcols[:, GC["gkva"] + l] = f(inp["mla_kv_a_norm"])[l]
        gcols[:96, GC["gq"] + l] = f(inp["mla_q_norm"])[l]
        gcols[:96, GC["gk"] + l] = f(inp["mla_k_norm"])[l]
        gcols[:64, GC["gdq"] + l] = f(inp["diff_q_norm"])[l]
        gcols[64:, GC["gdq"] + l] = f(inp["diff_q_norm"])[l]
        gcols[:64, GC["gdk"] + l] = f(inp["diff_k_norm"])[l]
        gcols[64:, GC["gdk"] + l] = f(inp["diff_k_norm"])[l]
        gcols[:, GC["gsub"] + l] = f(inp["diff_subln"])[l]
    lam = np.stack([f(inp["diff_lambda_q1"])[:L], f(inp["diff_lambda_k1"])[:L],
                    f(inp["diff_lambda_q2"])[:L], f(inp["diff_lambda_k2"])[:L]], 0)
    lamrep = np.ascontiguousarray(np.broadcast_to(lam.reshape(1, 4 * L * 64), (128, 4 * L * 64)))

    w_in = f(inp["w_in"])[:L]
    win_t = w_in.reshape(L, 8, 128, 1952).transpose(0, 2, 1, 3)
    WL = np.ascontiguousarray(win_t[:, :, :, 0:416]).reshape(L, 128, 8 * 416)
    WD = np.zeros((L, 4, 128, 8, 384), np.float32)
    for j in range(4):
        WD[:, j, :, :, 0:128] = win_t[:, :, :, 416 + 128 * j: 416 + 128 * j + 128]
        WD[:, j, :, :, 128:256] = win_t[:, :, :, 928 + 128 * j: 928 + 128 * j + 128]
        WD[:, j, :, :, 256:384] = win_t[:, :, :, 1440 + 128 * j: 1440 + 128 * j + 128]
    WD = WD.reshape(L, 4, 128, 8 * 384)
    wqb = f(inp["w_q_b"])[:L].reshape(L, 2, 128, 8, 96)
    wq = np.zeros((L, 128, 2, 8, 96), np.float32)
    wq[..., 0:32] = wqb.transpose(0, 2, 1, 3, 4)[..., 64:96]
    wq[..., 32:96] = wqb.transpose(0, 2, 1, 3, 4)[..., 0:64]
    wkvb = f(inp["w_kv_b"])[:L].reshape(L, 128, 8, 128)
    wk = np.zeros((L, 128, 8, 96), np.float32)
    wk[..., 32:96] = wkvb[..., 0:64]
    wv = wkvb[..., 64:128]
    WM = np.concatenate([wq.reshape(L, 128, 1536), wk.reshape(L, 128, 768), wv.reshape(L, 128, 512)], -1)
    WO = f(inp["w_o"])[:L].reshape(L, 8, 128, 1024)
    wg = f(inp["w_gate"])[:L].reshape(L, 8, 128, 11, 256).transpose(0, 3, 2, 1, 4)
    wu = f(inp["w_up"])[:L].reshape(L, 8, 128, 11, 256).transpose(0, 3, 2, 1, 4)
    WGU = np.ascontiguousarray(np.stack([wg, wu], 3)).reshape(L, 11, 128, 2 * 8 * 256)
    WDN = np.ascontiguousarray(
        f(inp["w_down"])[:L].reshape(L, 22, 128, 8, 128).transpose(0, 3, 2, 1, 4)).reshape(L, 8, 128, 22 * 128)

    cm = np.zeros((128, 5, 128), np.float32)
    cm[:, 0, :] = 1.0
    cm[0:64, 1, 0:64] = 1.0
    cm[64:128, 1, 64:128] = 1.0
    for i in range(16):
        cm[i + 16, 2, i] = -1.0
        cm[i, 2, i + 16] = 1.0
    for blk in (0, 64):
        for i in range(8):
            cm[blk + i + 8, 3, blk + i] = -1.0
            cm[blk + i, 3, blk + i + 8] = 1.0
    for i in range(32):
        cm[i, 4, i] = 1.0
    ident = np.eye(128, dtype=np.float32)
    ck = np.zeros((128, 4), np.float32)
    invM = np.exp(-math.log(MLA_THETA) * np.arange(16, dtype=np.float32) * (2.0 / 32)).astype(np.float32)
    invD = np.exp(-math.log(ROPE_THETA) * np.arange(8, dtype=np.float32) * (2.0 / 16)).astype(np.float32)
    for p in range(32):
        ck[p, 0] = invM[p % 16]; ck[p, 1] = 0.5 * math.pi
        ck[32 + p, 0] = invM[p % 16]; ck[32 + p, 1] = 0.0
    for p in range(16):
        ck[64 + p, 0] = invD[p % 8]; ck[64 + p, 1] = 0.5 * math.pi
        ck[96 + p, 0] = invD[p % 8]; ck[96 + p, 1] = 0.0
        ck[p, 2] = invD[p % 8]; ck[p, 3] = 0.5 * math.pi
    shared = {"gcols": gcols, "lamrep": lamrep, "WL": WL, "WD": WD, "WM": np.ascontiguousarray(WM),
              "WO": np.ascontiguousarray(WO), "WGU": WGU, "WDN": WDN,
              "cmat": cm.reshape(128, 5 * 128), "ident": ident, "ck": ck}
    return shared


def build_program(L=L_FULL, NB=B_PER_CORE, do_mla=True, do_diff=True, do_ffn=True):
    nc = bass.Bass("TRN2", target_bir_lowering=False)
    ng = _gc_layout(L)
    dt_in = lambda name, shape, dt=F32: nc.dram_tensor(name, shape, dt, kind="ExternalInput").ap()
    xin = dt_in("xin", [NB, S, D])
    posrep = dt_in("posrep", [NB, 128, S], I32)
    gcols_d = dt_in("gcols", [128, ng])
    lamrep_d = dt_in("lamrep", [128, 4 * L * 64])
    WL_d = dt_in("WL", [L, 128, 8 * 416])
    WD_d = dt_in("WD", [L, 4, 128, 8 * 384])
    WM_d = dt_in("WM", [L, 128, 2816])
    WO_d = dt_in("WO", [L, 8, 128, 1024])
    WGU_d = dt_in("WGU", [L, 11, 128, 4096])
    WDN_d = dt_in("WDN", [L, 8, 128, 2816])
    cmat_d = dt_in("cmat", [128, 5 * 128])
    ident_d = dt_in("ident", [128, 128])
    ck_d = dt_in("ck", [128, 4])
    yout = nc.dram_tensor("yout", [NB, S, D], F32, kind="ExternalOutput").ap()

    sb = nc.alloc_sbuf_tensor
    NPT, NATT = 3, 2
    xT = sb("xT", [128, KC, S], F32)
    NU = 16384 + 8192 + 12288 + NPT * 512 + NATT * 512
    U = sb("U", [128, NU], BF16)
    Wr = sb("Wr", [128, NWSLOT, WSLOT], BF16)
    TA = sb("TA", [128, S], F32)
    TB = sb("TB", [128, S], F32)
    NTF = 8
    tfb = sb("tfb", [128, NTF, TT], F32)
    o1b = sb("o1b", [128, 2, TT], F32)
    NTB = 4
    tbb = sb("tbb", [128, NTB, TT], BF16)
    cmat = sb("cmat_s", [128, 5, 128], BF16)
    ident = sb("ident_s", [128, 128], F32)
    gcols = sb("gcols_s", [128, ng], F32)
    ck = sb("ck_s", [128, 4], F32)
    lamc = sb("lamc", [128, 4 * L], F32)
    ps = nc.alloc_psum_tensor("ps", [128, 8, TT], F32)

    def uview(off, dims):
        n = int(np.prod(dims))
        ap = U[:, off:off + n]
        if len(dims) == 2:
            return ap.rearrange("p (a b) -> p a b", a=dims[0])
        return ap

    o = 0
    hT = uview(o, (KC, S)); o += KC * S
    cqn = uview(o, (2, S)); o += 2 * S
    ckvn = U[:, o:o + S]; o += S
    kpe = U[:, o:o + S]; o += S
    slots = []
    for s_ in range(2):
        qs = U[:, o:o + S]; o += S
        ks = U[:, o:o + S]; o += S
        vs = uview(o, (16, 128)); o += S
        slots.append((qs, ks, vs))
    PT = uview(o, (NPT, TT)); o += NPT * TT
    ATT = uview(o, (NATT, TT)); o += NATT * TT
    assert o == NU
    hTh = uview(0, (KC, 2 * TT))
    actT = uview(KC * 2 * TT, (NFC, 2 * TT))
    assert KC * 2 * TT + NFC * 2 * TT <= NU

    ONES = cmat[:, 0, :]
    BONES = cmat[:, 1, :]
    RM = cmat[:, 2, :]
    RD = cmat[:, 3, :]
    EM = cmat[:, 4, :]

    sc = Sched()
    A = sc.add

    def gcol(name, idx, p0=0, p1=128):
        c = GC[name] + idx
        return gcols[p0:p1, c:c + 1]

    cnt = {"bank": 0}
    tf_free = list(range(NTF))
    tb_free = list(range(NTB))

    def tf():
        assert tf_free, "out of f32 temps"
        i = tf_free.pop(0)
        return tfb[:, i, :], ("tf", i)

    def tb():
        assert tb_free, "out of bf16 temps"
        i = tb_free.pop(0)
        return tbb[:, i, :], ("tb", i)

    def rel(*keys):
        for k in keys:
            (tf_free if k[0] == "tf" else tb_free).append(k[1])

    misc_banks = [list(range(8))]

    def bank():
        lst = misc_banks[0]
        i = lst[cnt["bank"] % len(lst)]
        cnt["bank"] += 1
        return ps[:, i, :], ("ps", i)

    def mm(out, lhsT, rhs, start, stop, reads, writes):
        A("pe", lambda e: e.matmul(out, lhsT, rhs, start=start, stop=stop), reads, writes)

    def act(out, in_, func, reads, writes, scale=1.0, bias=0.0):
        A("act", lambda e: e.activation(out=out, in_=in_, func=func, scale=scale, bias=bias), reads, writes)

    def stt(out, in0, scalar, in1, op0, op1, reads, writes, eng="dve"):
        A(eng, lambda e: e.scalar_tensor_tensor(out=out, in0=in0, scalar=scalar, in1=in1, op0=op0, op1=op1),
          reads, writes)

    def tt(out, in0, in1, op, reads, writes, eng="dve"):
        A(eng, lambda e: e.tensor_tensor(out=out, in0=in0, in1=in1, op=op), reads, writes)

    def ts(out, in0, s1, s2, op0, op1, reads, writes, eng="dve"):
        if s2 is None:
            A(eng, lambda e: e.tensor_scalar(out=out, in0=in0, scalar1=s1, scalar2=None, op0=op0), reads, writes)
        else:
            A(eng, lambda e: e.tensor_scalar(out=out, in0=in0, scalar1=s1, scalar2=s2, op0=op0, op1=op1),
              reads, writes)

    def vcopy(out, in_, reads, writes, eng="dve"):
        if eng == "act":
            act(out, in_, AF.Copy, reads, writes)
        else:
            A(eng, lambda e: e.tensor_copy(out=out, in_=in_), reads, writes)

    def recip(out, in_, reads, writes):
        act(out, in_, AF.Ln, reads, writes)
        act(out, out, AF.Exp, writes, writes, scale=-1.0)

    wstate = {"free": list(range(NWSLOT)), "pending": deque()}

    class WBlock:
        def __init__(self, src, n):
            self.src, self.n, self.slot, self.key = src, n, None, None

    def _w_issue(blk):
        slot = wstate["free"].pop(0)
        blk.slot = slot
        blk.key = ("w", slot)
        dst = Wr[:, slot, 0:blk.n]
        src = blk.src
        sc.dma("pool", lambda e: e.dma_start(out=dst, in_=src), "w%d" % slot, reads=(), writes=(blk.key,))

    def w_prefetch(src, n):
        blk = WBlock(src, n)
        if wstate["free"] and not wstate["pending"]:
            _w_issue(blk)
        else:
            wstate["pending"].append(blk)
        return blk

    def w_use(blk):
        assert blk.slot is not None, "weight block not issued (ring too small for this schedule)"
        return Wr[:, blk.slot, :], blk.key

    def w_release(blk):
        wstate["free"].append(blk.slot)
        while wstate["free"] and wstate["pending"]:
            _w_issue(wstate["pending"].popleft())

    sc.dma("sp", lambda e: e.dma_start(out=gcols[:, :], in_=gcols_d), "c0", writes=("gcols",))
    sc.dma("sp", lambda e: e.dma_start(out=ident[:, :], in_=ident_d), "c1", writes=("ident",))
    sc.dma("sp", lambda e: e.dma_start(out=ck[:, :], in_=ck_d), "c2", writes=("ck",))
    sc.dma("pool", lambda e: e.dma_start(out=cmat[:, :, :], in_=cmat_d.rearrange("p (a b) -> p a b", a=5)),
           "c3", writes=("cmat",))
    lam_v = TA[:, 0:4 * L * 64]
    sc.dma("sp", lambda e: e.dma_start(out=lam_v, in_=lamrep_d), "c4", writes=("TA",))
    lam4 = lam_v.rearrange("p (a l d) -> p a l d", a=4, l=L)
    pr = TB[:, 0:2 * L * 64].rearrange("p (a l d) -> p a l d", a=2, l=L)
    tt(pr[:, 0, :, :], lam4[:, 0, :, :], lam4[:, 1, :, :], ALU.mult, ("TA",), ("TB",))
    tt(pr[:, 1, :, :], lam4[:, 2, :, :], lam4[:, 3, :, :], ALU.mult, ("TA",), ("TB",))
    sums = lamc[:, 2 * L:4 * L]
    A("dve", lambda e: e.tensor_reduce(out=sums, in_=TB[:, 0:2 * L * 64].rearrange("p (a d) -> p a d", d=64),
                                       axis=AX.X, op=ALU.add), ("TB",), ("lamc",))
    act(sums, sums, AF.Exp, ("lamc",), ("lamc",))
    tt(lamc[:, 0:L], lamc[:, 2 * L:3 * L], lamc[:, 3 * L:4 * L], ALU.subtract, ("lamc",), ("lamc",))
    for l in range(L):
        li = lambda_init(l)
        ts(lamc[:, l:l + 1], lamc[:, l:l + 1], -1.0, -li, ALU.mult, ALU.add, ("lamc",), ("lamc",))
        ts(lamc[:, L + l:L + l + 1], gcol("gsub", l), 1.0 - li, None, ALU.mult, None, ("gcols", "lamc"), ("lamc",))

    def nlam(l):
        return lamc[:, l:l + 1]

    def gsubs(l):
        return lamc[:, L + l:L + l + 1]

    st_in = [tfb[:, 0:2, :], tfb[:, 2:4, :]]
    st_keys = [[("tf", 0), ("tf", 1)], [("tf", 2), ("tf", 3)]]

    chain_live = [0]
    maxc = [3]

    def chain(fill_A, P, nrm_n, onesmat, gname, gidx, buf, bkey, t, kind):
        tsl = slice(t * TT, (t + 1) * TT)
        while chain_live[0] >= maxc[0]:
            yield 0.7
        chain_live[0] += 1
        Ak, akey = bank()
        fill_A(Ak, akey)
        qf, qfk = tf()
        vcopy(qf[0:P, :], Ak[0:P, :], (akey,), (qfk,))
        sq, sqk = tb()
        if kind == "mla":
            tt(sq[0:P, :], qf[0:P, :], qf[0:P, :], ALU.mult, (qfk,), (sqk,), eng="pool")
        else:
            act(sq[0:P, :], qf[0:P, :], AF.Square, (qfk,), (sqk,))
        PR = 32 if kind == "mla" else P
        ts(buf[0:PR, tsl], qf[0:PR, :], gcol(gname, gidx, 0, PR), None, ALU.mult, None,
           (qfk, "gcols"), (bkey,))
        yield (4.0 if kind == "mla" else 3.0)
        Bk, bkey2 = bank()
        mm(Bk[0:P, :], onesmat[0:P, 0:P], sq[0:P, :], True, True, (sqk, "cmat"), (bkey2,))
        Ck, ckey = bank()
        if kind == "mla":
            mm(Ck[0:32, :], RM[0:32, 0:32], buf[0:32, tsl], True, True, (bkey, "cmat"), (ckey,))
            segs = ((0, 32, TA, 0, "TA", TA, 32),)
        else:
            mm(Ck[:, :], RD[:, :], buf[:, tsl], True, True, (bkey, "cmat"), (ckey,))
            segs = ((0, 16, TB, 0, "TB", TA, 96), (64, 16, TA, 64, "TA", TA, 96))
        t1, t1k = tf()
        for (p0, n, cosT, c0, ckn, sinT, s0) in segs:
            tt(t1[p0:p0 + n, :], Ck[p0:p0 + n, :], sinT[s0:s0 + n, tsl], ALU.mult, (ckey, "TA"), (t1k,))
        rt, rtk = tf()
        act(rt[0:P, :], Bk[0:P, :], AF.Ln, (bkey2,), (rtk,), scale=1.0 / nrm_n, bias=EPS)
        act(rt[0:P, :], rt[0:P, :], AF.Exp, (rtk,), (rtk,), scale=-0.5)
        stt(buf[0:P, tsl], qf[0:P, :], gcol(gname, gidx, 0, P), rt[0:P, :], ALU.mult, ALU.mult,
            (qfk, rtk, "gcols"), (bkey,))
        t0, t0k = tf()
        for (p0, n, cosT, c0, ckn, sinT, s0) in segs:
            tt(t1[p0:p0 + n, :], t1[p0:p0 + n, :], rt[p0:p0 + n, :], ALU.mult, (t1k, rtk), (t1k,), eng="pool")
            tt(t0[p0:p0 + n, :], buf[p0:p0 + n, tsl], cosT[c0:c0 + n, tsl], ALU.mult, (bkey, ckn), (t0k,), eng="pool")
            tt(buf[p0:p0 + n, tsl], t0[p0:p0 + n, :], t1[p0:p0 + n, :], ALU.add, (t0k, t1k), (bkey,), eng="pool")
        rel(qfk, sqk, rtk, t1k, t0k)
        chain_live[0] -= 1
        yield 0.0

    def norm_tile(l, gname, t, dst, dkey_fn, dst_t):
        tsl = slice(t * TT, (t + 1) * TT)
        dsl = slice(dst_t * TT, (dst_t + 1) * TT)
        Bk, bkey = bank()
        for kc in range(KC):
            sq, sqk = tb()
            act(sq[:, :], xT[:, kc, tsl], AF.Square, (("xT", kc, t),), (sqk,))
            mm(Bk[:, :], ONES, sq[:, :], kc == 0, kc == KC - 1, (sqk, "cmat"), (bkey,))
            rel(sqk)
        rt, rtk = tf()
        act(rt[:, :], Bk[:, :], AF.Ln, (bkey,), (rtk,), scale=1.0 / D, bias=EPS)
        act(rt[:, :], rt[:, :], AF.Exp, (rtk,), (rtk,), scale=-0.5)
        for kc in range(KC):
            stt(dst[:, kc, dsl], xT[:, kc, tsl], gcol(gname, l * 8 + kc), rt[:, :], ALU.mult, ALU.mult,
                (("xT", kc, t), rtk, "gcols"), (dkey_fn(kc, dst_t),))
        rel(rtk)

    hkey = lambda kc, t: ("hT", kc, t)

    def latents(l):
        blk = w_prefetch(WL_d[l], 8 * 416)
        wl, wkey = w_use(blk)
        wl3 = wl[:, 0:8 * 416].rearrange("p (k c) -> p k c", k=8)
        for t in range(NT):
            tsl = slice(t * TT, (t + 1) * TT)
            raws = []
            for c in range(3):
                Ak, akey = bank()
                for kc in range(KC):
                    mm(Ak[:, :], wl3[:, kc, c * 128:(c + 1) * 128], hT[:, kc, tsl], kc == 0, kc == KC - 1,
                       (wkey, hkey(kc, t)), (akey,))
                raws.append((Ak, akey))
            Ak, akey = bank()
            for kc in range(KC):
                mm(Ak[0:32, :], wl3[:, kc, 384:416], hT[:, kc, tsl], kc == 0, kc == KC - 1,
                   (wkey, hkey(kc, t)), (akey,))
            act(kpe[0:32, tsl], Ak[0:32, :], AF.Copy, (akey,), (("kpe", t),))
            B0, b0k = bank()
            B1, b1k = bank()
            for c in range(3):
                sq, sqk = tb()
                act(sq[:, :], raws[c][0][:, :], AF.Square, (raws[c][1],), (sqk,))
                if c < 2:
                    mm(B0[:, :], ONES, sq[:, :], c == 0, c == 1, (sqk, "cmat"), (b0k,))
                else:
                    mm(B1[:, :], ONES, sq[:, :], True, True, (sqk, "cmat"), (b1k,))
                rel(sqk)
            r0, r0k = tf()
            act(r0[:, :], B0[:, :], AF.Ln, (b0k,), (r0k,), scale=1.0 / 256, bias=EPS)
            act(r0[:, :], r0[:, :], AF.Exp, (r0k,), (r0k,), scale=-0.5)
            r1, r1k = tf()
            act(r1[:, :], B1[:, :], AF.Ln, (b1k,), (r1k,), scale=1.0 / 128, bias=EPS)
            act(r1[:, :], r1[:, :], AF.Exp, (r1k,), (r1k,), scale=-0.5)
            for c in range(2):
                stt(cqn[:, c, tsl], raws[c][0][:, :], gcol("gqa", l * 2 + c), r0[:, :], ALU.mult, ALU.mult,
                    (raws[c][1], r0k, "gcols"), (("cqn", c, t),))
            stt(ckvn[:, tsl], raws[2][0][:, :], gcol("gkva", l), r1[:, :], ALU.mult, ALU.mult,
                (raws[2][1], r1k, "gcols"), (("ckvn", t),))
            rel(r0k, r1k)
            for _ in range(8):
                bg_step()
                bg_tick(1.0)
        w_release(blk)

    bg = {"tasks": [], "now": 0.0, "seq": 0}

    def bg_add(gen, prio, delay=0.0):
        bg["tasks"].append([prio, bg["seq"], bg["now"] + delay, gen])
        bg["seq"] += 1

    def bg_step():
        while True:
            cands = [t for t in bg["tasks"] if t[2] <= bg["now"] + 1e-9]
            if not cands:
                return False
            t = min(cands, key=lambda t_: (t_[0], t_[1]))
            try:
                d = next(t[3])
                t[2] = bg["now"] + (d or 0.0)
                return True
            except StopIteration:
                bg["tasks"].remove(t)

    def bg_tick(dt):
        bg["now"] += dt

    def bg_finish(max_prio_left):
        while any(t[0] >= max_prio_left for t in bg["tasks"]):
            if not bg_step():
                bg_tick(0.5)

    def counted(g, state, on_done):
        for d in g:
            yield d
        state[0] -= 1
        if state[0] == 0 and on_done is not None:
            on_done()

    def prep_mla_tasks(l, h, slot, wm, wmkey):
        qs, ks, vs = slots[slot]
        wq = wm[:, 0:1536].rearrange("p (k h d) -> p k h d", k=2, h=8)
        wk = wm[:, 1536:2304].rearrange("p (h d) -> p h d", h=8)
        wv = wm[:, 2304:2816].rearrange("p (h d) -> p h d", h=8)
        voff = 0 if h % 2 == 0 else 64
        ooff = 64 - voff
        A("pool", lambda e: e.memset(vs[:, :, ooff:ooff + 64], 1.0), (), (("vs", slot),))

        def fq(t):
            def f(Ak, akey):
                tsl = slice(t * TT, (t + 1) * TT)
                for kc in range(2):
                    mm(Ak[0:96, :], wq[:, kc, h, :], cqn[:, kc, tsl], kc == 0, kc == 1,
                       (wmkey, ("cqn", kc, t)), (akey,))
            return f

        def fk(t):
            def f(Ak, akey):
                tsl = slice(t * TT, (t + 1) * TT)
                mm(Ak[0:96, :], wk[:, h, :], ckvn[:, tsl], True, False, (wmkey, ("ckvn", t)), (akey,))
                mm(Ak[0:96, :], EM[0:32, 0:96], kpe[0:32, tsl], False, True, ("cmat", ("kpe", t)), (akey,))
            return f

        def vgen(t):
            Vk, vkey = bank()
            for j in range(4):
                mm(Vk[:, j * 64:(j + 1) * 64], ckvn[:, t * TT + j * 128:t * TT + (j + 1) * 128], wv[:, h, :],
                   True, True, (wmkey, ("ckvn", t)), (vkey,))
            vcopy(vs[:, 4 * t:4 * t + 4, voff:voff + 64], Vk[:, 0:256].rearrange("p (a b) -> p a b", a=4),
                  (vkey,), (("vs", slot),))
            yield 0.0

        for t in range(NT):
            bg_add(chain(fq(t), 96, 96.0, ONES, "gq", l, qs, ("qs", slot, t), t, "mla"), 1)
            bg_add(chain(fk(t), 96, 96.0, ONES, "gk", l, ks, ("ks", slot, t), t, "mla"), 1)
            bg_add(vgen(t), 1)

    def prep_diff_tasks(l, j, slot):
        qs, ks, vs = slots[slot]
        blk = w_prefetch(WD_d[l, j], 8 * 384)
        state = [3 * NT]

        def wd3():
            wd, wkey = w_use(blk)
            return wd[:, 0:8 * 384].rearrange("p (k c) -> p k c", k=8), wkey

        def fqk(which, t):
            def f(Ak, akey):
                w3, wkey = wd3()
                tsl = slice(t * TT, (t + 1) * TT)
                for kc in range(KC):
                    mm(Ak[:, :], w3[:, kc, which * 128:(which + 1) * 128], hT[:, kc, tsl], kc == 0, kc == KC - 1,
                       (wkey, hkey(kc, t)), (akey,))
            return f

        def vgen(t):
            w3, wkey = wd3()
            Vk, vkey = bank()
            for j4 in range(4):
                for kc in range(KC):
                    mm(Vk[:, j4 * 128:(j4 + 1) * 128], hT[:, kc, t * TT + j4 * 128:t * TT + (j4 + 1) * 128],
                       w3[:, kc, 256:384], kc == 0, kc == KC - 1, (wkey, hkey(kc, t)), (vkey,))
            vcopy(vs[:, 4 * t:4 * t + 4, :], Vk[:, :].rearrange("p (a b) -> p a b", a=4), (vkey,), (("vs", slot),))
            yield 0.0

        rel_blk = lambda: w_release(blk)
        for t in range(NT):
            bg_add(counted(chain(fqk(0, t), 128, 64.0, BONES, "gdq", l, qs, ("qs", slot, t), t, "diff"),
                           state, rel_blk), 1)
            bg_add(counted(chain(fqk(1, t), 128, 64.0, BONES, "gdk", l, ks, ("ks", slot, t), t, "diff"),
                           state, rel_blk), 1)
            bg_add(counted(vgen(t), state, rel_blk), 1)

    att_cnt = {"pt": 0, "att": 0, "s": 0}
    busy = {}

    def att_alloc():
        ai = att_cnt["att"] % NATT
        att_cnt["att"] += 1
        assert not busy.get(("att", ai)), "att buffer still busy"
        busy[("att", ai)] = True
        return ai

    def wo_task(l, wo, wokey, krows, att_ap, attkey, qt, release_blk=None):
        p0, p1 = krows
        for m in range(KC):
            Yk, ykey = bank()
            mm(Yk[:, :], wo[p0:p1, m * 128:(m + 1) * 128], att_ap[p0:p1, :], True, True, (wokey, attkey), (ykey,))
            tsl = slice(qt * TT, (qt + 1) * TT)
            tt(xT[:, m, tsl], Yk[:, :], xT[:, m, tsl], ALU.add, (ykey, ("xT", m, qt)), (("xT", m, qt),))
            yield 0.0
        busy[attkey] = False
        if release_blk is not None:
            w_release(release_blk)

    ATTP = o1b[:, :, :].rearrange("p a b -> p (a b)").bitcast(BF16).rearrange("p (a b) -> p a b", a=4)

    def attention_head(kind, l, h, slot, scale, obanks, wo, wokey, release_blk):
        qs, ks, vs = slots[slot]
        nmap = 1 if kind == "mla" else 2
        Kr = 96 if kind == "mla" else 64
        dt_iter = 0.64 if kind == "mla" else 0.86
        iters = []
        for qt in range(NT):
            for mp in range(nmap):
                for kt in range(16):
                    iters.append((qt, mp, kt))
        LOOK = 2
        pend = deque()
        SB = (0, 1, 2)
        for i in range(len(iters) + LOOK):
            if i < len(iters):
                qt, mp, kt = iters[i]
                sbk = SB[att_cnt["s"] % 3]
                att_cnt["s"] += 1
                pti = att_cnt["pt"] % NPT
                att_cnt["pt"] += 1
                p0 = mp * 64 if kind == "diff" else 0
                qsl = slice(qt * TT, (qt + 1) * TT)
                mm(ps[:, sbk, :], ks[p0:p0 + Kr, kt * 128:(kt + 1) * 128], qs[p0:p0 + Kr, qsl], True, True,
                   (("ks", slot, kt // 4), ("qs", slot, qt)), (("ps", sbk),))
                act(PT[:, pti, :], ps[:, sbk, :], AF.Exp, (("ps", sbk),), (("pt", pti),), scale=scale)
                pend.append((qt, mp, kt, pti))
            if i >= LOOK:
                qt, mp, kt, pti = pend.popleft()
                if kind == "mla":
                    ob = obanks[qt % 2]
                    mm(ps[:, ob, :], vs[:, kt, :], PT[:, pti, :], kt == 0, kt == 15,
                       (("vs", slot), ("pt", pti)), (("ps", ob),))
                    if kt == 15:
                        orow = 0 if h % 2 == 0 else 64
                        srow = 64 - orow
                        rr, rrk = tf()
                        recip(rr[srow:srow + 64, :], ps[srow:srow + 64, ob, :], (("ps", ob),), (rrk,))
                        tt(ATTP[orow:orow + 64, qt, :], ps[orow:orow + 64, ob, :], rr[srow:srow + 64, :], ALU.mult,
                           (("ps", ob), rrk), (("attp", qt),))
                        rel(rrk)
                        last = (qt == NT - 1)
                        if h % 2 == 1:
                            bg_add(wo_task(l, wo, wokey, (0, 128), ATTP[:, qt, :], ("attp", qt), qt,
                                           release_blk if last else None), 0, delay=(3.0 if qt < 2 else 11.0))
                else:
                    ob, zb = obanks
                    mm(ps[:, ob, :], vs[:, kt, :], PT[:, pti, :], kt == 0, kt == 15,
                       (("vs", slot), ("pt", pti)), (("ps", ob),))
                    mm(ps[:, zb, :], ONES, PT[:, pti, :], kt == 0, kt == 15, (("pt", pti), "cmat"), (("ps", zb),))
                    if kt == 15:
                        oc, ock = tf()
                        vcopy(oc[:, :], ps[:, ob, :], (("ps", ob),), (ock,))
                        rr, rrk = tf()
                        recip(rr[:, :], ps[:, zb, :], (("ps", zb),), (rrk,))
                        dd, ddk = o1b[:, qt % 2, :], ("o1", qt % 2)
                        if mp == 0:
                            assert not busy.get(ddk), "o1 buffer still busy"
                            busy[ddk] = True
                            tt(dd, oc[:, :], rr[:, :], ALU.mult, (ock, rrk), (ddk,))
                            rel(rrk, ock)
                        else:
                            tt(oc[:, :], oc[:, :], rr[:, :], ALU.mult, (ock, rrk), (ock,))
                            stt(dd, oc[:, :], nlam(l), dd, ALU.mult, ALU.add, (ock, ddk, "lamc"), (ddk,))
                            rel(rrk, ock)
                            sq, sqk = tb()
                            act(sq[:, :], dd, AF.Square, (ddk,), (sqk,))
                            last = (qt == NT - 1)
                            bg_add(diff_post(l, dd, ddk, sq, sqk, qt, wo, wokey, release_blk if last else None), 0,
                                   delay=4.5)
            bg_step()
            bg_tick(dt_iter)

    def diff_post(l, dd, ddk, sq, sqk, qt, wo, wokey, release_blk):
        Bk, bkey = bank()
        mm(Bk[:, :], ONES, sq[:, :], True, True, (sqk, "cmat"), (bkey,))
        rt, rtk = tf()
        act(rt[:, :], Bk[:, :], AF.Ln, (bkey,), (rtk,), scale=1.0 / 128, bias=EPS)
        act(rt[:, :], rt[:, :], AF.Exp, (rtk,), (rtk,), scale=-0.5)
        ai = att_alloc()
        stt(ATT[:, ai, :], dd, gsubs(l), rt[:, :], ALU.mult, ALU.mult, (ddk, rtk, "lamc"), (("att", ai),))
        rel(rtk, sqk)
        busy[ddk] = False
        yield 0.0
        for d in wo_task(l, wo, wokey, (0, 128), ATT[:, ai, :], ("att", ai), qt, release_blk):
            yield d

    def attention_layer(l, pre_registered_diff0):
        wmb = None
        wm = wmkey = None
        if do_diff:
            misc_banks[0] = [5, 6, 7]
            if not pre_registered_diff0:
                prep_diff_tasks(l, 0, 0)
            bg_finish(1)
            for j in range(4):
                blk = w_prefetch(WO_d[l, 4 + j], 1024)
                if j + 1 < 4:
                    prep_diff_tasks(l, j + 1, (j + 1) % 2)
                elif do_mla:
                    wmb = w_prefetch(WM_d[l], 2816)
                    wm, wmkey = w_use(wmb)
                    prep_mla_tasks(l, 0, 0, wm, wmkey)
                wo, wokey = w_use(blk)
                attention_head("diff", l, j, j % 2, 1.0 / math.sqrt(64.0), (3, 4), wo, wokey, blk)
                bg_finish(1)
        if do_mla:
            bg_finish(0)
            w_ = [t for k in (("o1", 0), ("o1", 1)) for t in ([sc.w.get(k)] + list(sc.r.get(k, []))) if t is not None]
            sc.ops["dve"].append({"fn": None, "waits": [t for t in w_ if sc._need("dve", t, True)],
                                  "signal": False, "dma": None})
            misc_banks[0] = [4, 6, 7]
            if wmb is None:
                wmb = w_prefetch(WM_d[l], 2816)
                wm, wmkey = w_use(wmb)
                prep_mla_tasks(l, 0, 0, wm, wmkey)
                bg_finish(1)
            blk = None
            for h in range(8):
                if h % 2 == 0:
                    blk = w_prefetch(WO_d[l, h // 2], 1024)
                if h + 1 < 8:
                    prep_mla_tasks(l, h + 1, (h + 1) % 2, wm, wmkey)
                wo, wokey = w_use(blk)
                attention_head("mla", l, h, h % 2, 1.0 / math.sqrt(96.0), (3, 5), wo, wokey,
                               blk if h % 2 == 1 else None)
                bg_finish(1)
        bg_finish(0)
        if wmb is not None:
            w_release(wmb)
        misc_banks[0] = list(range(8))

    def ffn_layer(l):
        for hf in range(2):
            misc_banks[0] = [6, 7]
            for tt_ in range(2):
                norm_tile(l, "g2", 2 * hf + tt_, hTh, lambda kc, t: ("hTh", kc, t), tt_)
            blks = deque()
            seq = [("gu", cp) for cp in range(11)] + [("dn", m) for m in range(8)]

            def pf(i):
                kind, idx = seq[i]
                if kind == "gu":
                    return w_prefetch(WGU_d[l, idx], 4096)
                return w_prefetch(WDN_d[l, idx], 2816)
            nxt = 0
            while nxt < min(2, len(seq)):
                blks.append(pf(nxt)); nxt += 1
            gub = 0
            for i, (kind, idx) in enumerate(seq):
                blk = blks.popleft()
                w, wkey = w_use(blk)
                if kind == "gu":
                    w4 = w[:, 0:4096].rearrange("p (a k c) -> p a k c", a=2, k=8)
                    for cc in range(2):
                        c = 2 * idx + cc
                        for tt_ in range(2):
                            gb = (gub % 2) * 2
                            gub += 1
                            dsl = slice(tt_ * TT, (tt_ + 1) * TT)
                            for a_ in range(2):
                                for kc in range(KC):
                                    mm(ps[:, gb + a_, :], w4[:, a_, kc, cc * 128:(cc + 1) * 128], hTh[:, kc, dsl],
                                       kc == 0, kc == KC - 1, (wkey, ("hTh", kc, tt_)), (("ps", gb + a_),))
                            sg, sgk = tf()
                            act(sg[:, :], ps[:, gb, :], AF.Silu, (("ps", gb),), (sgk,))
                            tt(actT[:, c, dsl], ps[:, gb + 1, :], sg[:, :], ALU.mult, (sgk, ("ps", gb + 1)),
                               (("actT", c, tt_),))
                            rel(sgk)
                else:
                    w3 = w[:, 0:2816].rearrange("p (c m) -> p c m", c=22)
                    for tt_ in range(2):
                        yb = 4 + (idx * 2 + tt_) % 2
                        dsl = slice(tt_ * TT, (tt_ + 1) * TT)
                        for c in range(NFC):
                            mm(ps[:, yb, :], w3[:, c, :], actT[:, c, dsl], c == 0, c == NFC - 1,
                               (wkey, ("actT", c, tt_)), (("ps", yb),))
                        t = 2 * hf + tt_
                        tsl = slice(t * TT, (t + 1) * TT)
                        tt(xT[:, idx, tsl], ps[:, yb, :], xT[:, idx, tsl], ALU.add,
                           (("ps", yb), ("xT", idx, t)), (("xT", idx, t),))
                w_release(blk)
                if nxt < len(seq):
                    blks.append(pf(nxt)); nxt += 1
        misc_banks[0] = list(range(8))

    def load_batch(b):
        scrF = U[:, 0:2 * S].bitcast(F32)
        scrI = U[:, 0:2 * S].bitcast(I32)
        C1 = 6.28125
        C2 = 2.0 * math.pi - C1
        sc.dma("sp", lambda e: e.dma_start(out=TB[:, :].bitcast(I32), in_=posrep[b]), "c5", writes=("TB",))
        vcopy(TA[:, :], TB[:, :].bitcast(I32), ("TB",), ("TA",))
        ts(TB[:, :], TA[:, :], ck[:, 2:3], ck[:, 3:4], ALU.mult, ALU.add, ("TA", "ck"), ("TB",))
        ts(TA[:, :], TA[:, :], ck[:, 0:1], ck[:, 1:2], ALU.mult, ALU.add, ("TA", "ck"), ("TA",))
        for (T_, k_) in ((TA, "TA"), (TB, "TB")):
            ts(scrF, T_[:, :], 1.0 / (2.0 * math.pi), None, ALU.mult, None, (k_,), ("Uscr",))
            vcopy(scrI, scrF, ("Uscr",), ("Uscr",))
            vcopy(scrF, scrI, ("Uscr",), ("Uscr",))
            stt(T_[:, :], scrF, -C1, T_[:, :], ALU.mult, ALU.add, ("Uscr", k_), (k_,))
            stt(T_[:, :], scrF, -C2, T_[:, :], ALU.mult, ALU.add, ("Uscr", k_), (k_,))
            ts(scrF, T_[:, :], math.pi, 2.0 * math.pi, ALU.is_gt, ALU.mult, (k_,), ("Uscr",))
            tt(T_[:, :], T_[:, :], scrF, ALU.subtract, (k_, "Uscr"), (k_,))
            ts(T_[:, :], T_[:, :], -math.pi, math.pi, ALU.max, ALU.min, (k_,), (k_,))
            act(T_[:, :], T_[:, :], AF.Sin, (k_,), (k_,))
        for i in range(S // 128):
            si = i % 2
            stg = st_in[si]
            sc.dma("sp", lambda e, stg=stg, i=i: e.dma_start(
                out=stg, in_=xin[b, i * 128:(i + 1) * 128, :].rearrange("p (a c) -> p a c", a=2)),
                "st%d" % si, writes=tuple(st_keys[si]))
            stf = [tfb[:, 2 * si, :], tfb[:, 2 * si + 1, :]]
            for half in range(2):
                Bk, bkey = bank()
                for q4 in range(4):
                    kc = half * 4 + q4
                    src = stf[kc // 4][:, (kc % 4) * 128:(kc % 4 + 1) * 128]
                    A("pe", lambda e, o_=Bk[:, q4 * 128:(q4 + 1) * 128], s_=src: e.transpose(o_, s_, ident[:, :]),
                      (st_keys[si][kc // 4], "ident"), (bkey,))
                t = i // 4
                vcopy(xT[:, half * 4:half * 4 + 4, i * 128:(i + 1) * 128],
                      Bk[:, :].rearrange("p (a c) -> p a c", a=4), (bkey,),
                      tuple(("xT", half * 4 + q, t) for q in range(4)), eng=("dve" if half == 0 else "act"))

    def store_batch(b):
        for i in range(S // 128):
            si = i % 2
            stf = [tfb[:, 2 * si, :], tfb[:, 2 * si + 1, :]]
            t = i // 4
            for half in range(2):
                Bk, bkey = bank()
                for q4 in range(4):
                    kc = half * 4 + q4
                    A("pe", lambda e, o_=Bk[:, q4 * 128:(q4 + 1) * 128], s_=xT[:, kc, i * 128:(i + 1) * 128]:
                      e.transpose(o_, s_, ident[:, :]), (("xT", kc, t), "ident"), (bkey,))
                vcopy(stf[half][:, :], Bk[:, :], (bkey,), (st_keys[si][half],), eng=("dve" if half == 0 else "act"))
            stg = st_in[si]
            sc.dma("sp", lambda e, stg=stg, i=i: e.dma_start(
                out=yout[b, i * 128:(i + 1) * 128, :].rearrange("p (a c) -> p a c", a=2), in_=stg),
                "so%d" % si, reads=tuple(st_keys[si]))

    for b in range(NB):
        sc.barrier()
        load_batch(b)
        for l in range(L):
            sc.barrier()
            misc_banks[0] = list(range(8))
            for t in range(NT):
                norm_tile(l, "g1", t, hT, hkey, t)
            pre = False
            if do_diff:
                prep_diff_tasks(l, 0, 0)
                pre = True
            if do_mla:
                latents(l)
            attention_layer(l, pre)
            sc.barrier()
            if do_ffn:
                ffn_layer(l)
        sc.barrier()
        store_batch(b)
    sc.wait_all_dma("sp", ["so0", "so1"])
    sc.emit(nc)
    return nc, sc


def make_in_maps(inp, L, ncores, nb):
    shared = host_prepare(inp, L)
    x = np.ascontiguousarray(np.asarray(inp["x"]), dtype=np.float32)
    pos = np.ascontiguousarray(np.asarray(inp["positions"]), dtype=np.int32)
    in_maps = []
    for c in range(ncores):
        m = dict(shared)
        m["xin"] = np.ascontiguousarray(x[c * nb:(c + 1) * nb])
        m["posrep"] = np.ascontiguousarray(np.broadcast_to(pos[c * nb:(c + 1) * nb, None, :], (nb, 128, S)))
        in_maps.append(m)
    return in_maps


_CACHE = {}


def kernel(**inputs):
    key = "full"
    if key not in _CACHE:
        _CACHE[key] = build_program(L_FULL, B_PER_CORE)[0]
    nc = _CACHE[key]
    in_maps = make_in_maps(inputs, L_FULL, NCORES, B_PER_CORE)
    res = run_bass_kernel_spmd(nc, in_maps, core_ids=list(range(NCORES)))
    out = np.concatenate([np.asarray(r["yout"]) for r in res.results], axis=0)
    return out.astype(np.float32, copy=False)
```

```python
import math
from collections import deque

import numpy as np
import concourse.bass as bass
import concourse.mybir as mybir
from concourse.bass_utils import run_bass_kernel_spmd

F32 = mybir.dt.float32
BF16 = mybir.dt.bfloat16
I32 = mybir.dt.int32
AF = mybir.ActivationFunctionType
ALU = mybir.AluOpType
AX = mybir.AxisListType

D = 1024
KC = 8
S = 2048
TT = 512
NT = S // TT
L_FULL = 4
NCORES = 8
B_PER_CORE = 2
FF = 2816
NFC = FF // 128
EPS = 1e-6
MLA_THETA = 10000.0
ROPE_THETA = 500000.0
WL_COLS = 416
WSLOT = 4096
NWSLOT = 3


def lambda_init(l):
    return 0.8 - 0.6 * math.exp(-0.3 * l)


class Sched:
    ENGS = ("pe", "act", "dve", "pool", "sp")

    def __init__(self):
        self.ops = {e: [] for e in self.ENGS}
        self.w = {}
        self.r = {}
        self.seen = {e: {} for e in self.ENGS}
        self.dma_val = {}

    def _need(self, eng, tok, is_raw):
        if tok is None:
            return False
        kind, src, val = tok
        if kind == "E" and src == eng:
            if eng == "pe" or eng == "sp":
                return False
        seen = self.seen[eng]
        k = (kind, src)
        if seen.get(k, -1) >= val:
            return False
        seen[k] = val
        return True

    def _deps(self, eng, reads, writes):
        waits = []
        for k in reads:
            t = self.w.get(k)
            if self._need(eng, t, True):
                waits.append(t)
        for k in writes:
            t = self.w.get(k)
            if self._need(eng, t, True):
                waits.append(t)
            for t in self.r.get(k, ()):
                if self._need(eng, t, False):
                    waits.append(t)
        return waits

    def add(self, eng, fn, reads=(), writes=()):
        waits = self._deps(eng, reads, writes)
        idx = len(self.ops[eng])
        tok = ("E", eng, idx)
        self.ops[eng].append({"fn": fn, "waits": waits, "signal": False, "dma": None})
        for k in writes:
            self.w[k] = tok
            self.r[k] = []
        for k in reads:
            self.r.setdefault(k, []).append(tok)
        return tok

    def dma(self, queue, fn, sem, reads=(), writes=()):
        waits = self._deps(queue, reads, writes)
        v = self.dma_val.get(sem, 0) + 16
        self.dma_val[sem] = v
        tok = ("D", sem, v)
        self.ops[queue].append({"fn": fn, "waits": waits, "signal": False, "dma": sem})
        for k in writes:
            self.w[k] = tok
            self.r[k] = []
        for k in reads:
            self.r.setdefault(k, []).append(tok)
        return tok

    def wait_all_dma(self, eng, sems):
        waits = [("D", s, self.dma_val[s]) for s in sems if self.dma_val.get(s, 0) > 0]
        self.ops[eng].append({"fn": None, "waits": waits, "signal": False, "dma": None})

    def barrier(self, engs=("pe", "act", "dve")):
        last = {}
        for e in engs:
            i = len(self.ops[e]) - 1
            while i >= 0 and (self.ops[e][i]["fn"] is None or self.ops[e][i]["dma"] is not None):
                i -= 1
            last[e] = i
        for e in engs:
            waits = []
            for o in engs:
                if o == e or last[o] < 0:
                    continue
                t = ("E", o, last[o])
                if self._need(e, t, True):
                    waits.append(t)
            if waits:
                self.ops[e].append({"fn": None, "waits": waits, "signal": False, "dma": None})

    def emit(self, nc):
        for e in self.ENGS:
            for op in self.ops[e]:
                for (kind, src, val) in op["waits"]:
                    if kind == "E":
                        self.ops[src][val]["signal"] = True
        sigval = {}
        for e in self.ENGS:
            c = 0
            for i, op in enumerate(self.ops[e]):
                if op["signal"]:
                    c += 1
                    sigval[(e, i)] = c
        dma_sems = sorted(self.dma_val.keys())
        import contextlib
        with contextlib.ExitStack() as st:
            esem = {e: st.enter_context(nc.semaphore("e_" + e)) for e in self.ENGS}
            dsem = {s: st.enter_context(nc.semaphore("d_" + s)) for s in dma_sems}
            block = st.enter_context(nc.Block())

            def run(eng_name):
                def body(eng):
                    for i, op in enumerate(self.ops[eng_name]):
                        for (kind, src, val) in op["waits"]:
                            if kind == "E":
                                eng.wait_ge(esem[src], sigval[(src, val)])
                            else:
                                eng.wait_ge(dsem[src], val)
                        if op["fn"] is None:
                            continue
                        ins = op["fn"](eng)
                        if op["dma"] is not None:
                            ins.then_inc(dsem[op["dma"]], 16)
                        elif op["signal"]:
                            ins.then_inc(esem[eng_name], 1)
                return body

            block.tensor(run("pe"))
            block.scalar(run("act"))
            block.vector(run("dve"))
            block.gpsimd(run("pool"))
            block.sync(run("sp"))


GC = {}


def _gc_layout(L):
    off = 0
    for name, n in (("g1", L * 8), ("g2", L * 8), ("gqa", L * 2), ("gkva", L), ("gq", L),
                    ("gk", L), ("gdq", L), ("gdk", L), ("gsub", L)):
        GC[name] = off
        off += n
    return off


def host_prepare(inp, L):
    f = lambda a: np.ascontiguousarray(np.asarray(a), dtype=np.float32)
    ng = _gc_layout(L)
    gcols = np.zeros((128, ng), np.float32)
    an, fn_ = f(inp["attn_norm"]), f(inp["ffn_norm"])
    for l in range(L):
        gcols[:, GC["g1"] + l * 8: GC["g1"] + l * 8 + 8] = an[l].reshape(8, 128).T
        gcols[:, GC["g2"] + l * 8: GC["g2"] + l * 8 + 8] = fn_[l].reshape(8, 128).T
        gcols[:, GC["gqa"] + l * 2: GC["gqa"] + l * 2 + 2] = f(inp["mla_q_a_norm"])[l].reshape(2, 128).T
        gcols[:, GC["gkva"] + l] = f(inp["mla_kv_a_norm"])[l]
        gcols[:96, GC["gq"] + l] = f(inp["mla_q_norm"])[l]
        gcols[:96, GC["gk"] + l] = f(inp["mla_k_norm"])[l]
        gcols[:64, GC["gdq"] + l] = f(inp["diff_q_norm"])[l]
        gcols[64:, GC["gdq"] + l] = f(inp["diff_q_norm"])[l]
        gcols[:64, GC["gdk"] + l] = f(inp["diff_k_norm"])[l]
        gcols[64:, GC["gdk"] + l] = f(inp["diff_k_norm"])[l]
        gcols[:, GC["gsub"] + l] = f(inp["diff_subln"])[l]
    lam = np.stack([f(inp["diff_lambda_q1"])[:L], f(inp["diff_lambda_k1"])[:L],
                    f(inp["diff_lambda_q2"])[:L], f(inp["diff_lambda_k2"])[:L]], 0)
    lamrep = np.ascontiguousarray(np.broadcast_to(lam.reshape(1, 4 * L * 64), (128, 4 * L * 64)))

    w_in = f(inp["w_in"])[:L]
    win_t = w_in.reshape(L, 8, 128, 1952).transpose(0, 2, 1, 3)
    WL = np.ascontiguousarray(win_t[:, :, :, 0:416]).reshape(L, 128, 8 * 416)
    WD = np.zeros((L, 4, 128, 8, 384), np.float32)
    for j in range(4):
        WD[:, j, :, :, 0:128] = win_t[:, :, :, 416 + 128 * j: 416 + 128 * j + 128]
        WD[:, j, :, :, 128:256] = win_t[:, :, :, 928 + 128 * j: 928 + 128 * j + 128]
        WD[:, j, :, :, 256:384] = win_t[:, :, :, 1440 + 128 * j: 1440 + 128 * j + 128]
    WD = WD.reshape(L, 4, 128, 8 * 384)
    wqb = f(inp["w_q_b"])[:L].reshape(L, 2, 128, 8, 96)
    wq = np.zeros((L, 128, 2, 8, 96), np.float32)
    wq[..., 0:32] = wqb.transpose(0, 2, 1, 3, 4)[..., 64:96]
    wq[..., 32:96] = wqb.transpose(0, 2, 1, 3, 4)[..., 0:64]
    wkvb = f(inp["w_kv_b"])[:L].reshape(L, 128, 8, 128)
    wk = np.zeros((L, 128, 8, 96), np.float32)
    wk[..., 32:96] = wkvb[..., 0:64]
    wv = wkvb[..., 64:128]
    WM = np.concatenate([wq.reshape(L, 128, 1536), wk.reshape(L, 128, 768), wv.reshape(L, 128, 512)], -1)
    WO = f(inp["w_o"])[:L].reshape(L, 8, 128, 1024)
    wg = f(inp["w_gate"])[:L].reshape(L, 8, 128, 11, 256).transpose(0, 3, 2, 1, 4)
    wu = f(inp["w_up"])[:L].reshape(L, 8, 128, 11, 256).transpose(0, 3, 2, 1, 4)
    WGU = np.ascontiguousarray(np.stack([wg, wu], 3)).reshape(L, 11, 128, 2 * 8 * 256)
    WDN = np.ascontiguousarray(
        f(inp["w_down"])[:L].reshape(L, 22, 128, 8, 128).transpose(0, 3, 2, 1, 4)).reshape(L, 8, 128, 22 * 128)

    cm = np.zeros((128, 5, 128), np.float32)
    cm[:, 0, :] = 1.0
    cm[0:64, 1, 0:64] = 1.0
    cm[64:128, 1, 64:128] = 1.0
    for i in range(16):
        cm[i + 16, 2, i] = -1.0
        cm[i, 2, i + 16] = 1.0
    for blk in (0, 64):
        for i in range(8):
            cm[blk + i + 8, 3, blk + i] = -1.0
            cm[blk + i, 3, blk + i + 8] = 1.0
    for i in range(32):
        cm[i, 4, i] = 1.0
    ident = np.eye(128, dtype=np.float32)
    ck = np.zeros((128, 4), np.float32)
    invM = np.exp(-math.log(MLA_THETA) * np.arange(16, dtype=np.float32) * (2.0 / 32)).astype(np.float32)
    invD = np.exp(-math.log(ROPE_THETA) * np.arange(8, dtype=np.float32) * (2.0 / 16)).astype(np.float32)
    for p in range(32):
        ck[p, 0] = invM[p % 16]; ck[p, 1] = 0.5 * math.pi
        ck[32 + p, 0] = invM[p % 16]; ck[32 + p, 1] = 0.0
    for p in range(16):
        ck[64 + p, 0] = invD[p % 8]; ck[64 + p, 1] = 0.5 * math.pi
        ck[96 + p, 0] = invD[p % 8]; ck[96 + p, 1] = 0.0
        ck[p, 2] = invD[p % 8]; ck[p, 3] = 0.5 * math.pi
    shared = {"gcols": gcols, "lamrep": lamrep, "WL": WL, "WD": WD, "WM": np.ascontiguousarray(WM),
              "WO": np.ascontiguousarray(WO), "WGU": WGU, "WDN": WDN,
              "cmat": cm.reshape(128, 5 * 128), "ident": ident, "ck": ck}
    return shared


def build_program(L=L_FULL, NB=B_PER_CORE, do_mla=True, do_diff=True, do_ffn=True):
    nc = bass.Bass("TRN2", target_bir_lowering=False)
    ng = _gc_layout(L)
    dt_in = lambda name, shape, dt=F32: nc.dram_tensor(name, shape, dt, kind="ExternalInput").ap()
    xin = dt_in("xin", [NB, S, D])
    posrep = dt_in("posrep", [NB, 128, S], I32)
    gcols_d = dt_in("gcols", [128, ng])
    lamrep_d = dt_in("lamrep", [128, 4 * L * 64])
    WL_d = dt_in("WL", [L, 128, 8 * 416])
    WD_d = dt_in("WD", [L, 4, 128, 8 * 384])
    WM_d = dt_in("WM", [L, 128, 2816])
    WO_d = dt_in("WO", [L, 8, 128, 1024])
    WGU_d = dt_in("WGU", [L, 11, 128, 4096])
    WDN_d = dt_in("WDN", [L, 8, 128, 2816])
    cmat_d = dt_in("cmat", [128, 5 * 128])
    ident_d = dt_in("ident", [128, 128])
    ck_d = dt_in("ck", [128, 4])
    yout = nc.dram_tensor("yout", [NB, S, D], F32, kind="ExternalOutput").ap()

    sb = nc.alloc_sbuf_tensor
    NPT, NATT = 3, 2
    xT = sb("xT", [128, KC, S], F32)
    NU = 16384 + 8192 + 12288 + NPT * 512 + NATT * 512
    U = sb("U", [128, NU], BF16)
    Wr = sb("Wr", [128, NWSLOT, WSLOT], BF16)
    TA = sb("TA", [128, S], F32)
    TB = sb("TB", [128, S], F32)
    NTF = 8
    tfb = sb("tfb", [128, NTF, TT], F32)
    o1b = sb("o1b", [128, 2, TT], F32)
    NTB = 4
    tbb = sb("tbb", [128, NTB, TT], BF16)
    cmat = sb("cmat_s", [128, 5, 128], BF16)
    ident = sb("ident_s", [128, 128], F32)
    gcols = sb("gcols_s", [128, ng], F32)
    ck = sb("ck_s", [128, 4], F32)
    lamc = sb("lamc", [128, 4 * L], F32)
    ps = nc.alloc_psum_tensor("ps", [128, 8, TT], F32)

    def uview(off, dims):
        n = int(np.prod(dims))
        ap = U[:, off:off + n]
        if len(dims) == 2:
            return ap.rearrange("p (a b) -> p a b", a=dims[0])
        return ap

    o = 0
    hT = uview(o, (KC, S)); o += KC * S
    cqn = uview(o, (2, S)); o += 2 * S
    ckvn = U[:, o:o + S]; o += S
    kpe = U[:, o:o + S]; o += S
    slots = []
    for s_ in range(2):
        qs = U[:, o:o + S]; o += S
        ks = U[:, o:o + S]; o += S
        vs = uview(o, (16, 128)); o += S
        slots.append((qs, ks, vs))
    PT = uview(o, (NPT, TT)); o += NPT * TT
    ATT = uview(o, (NATT, TT)); o += NATT * TT
    assert o == NU
    hTh = uview(0, (KC, 2 * TT))
    actT = uview(KC * 2 * TT, (NFC, 2 * TT))
    assert KC * 2 * TT + NFC * 2 * TT <= NU

    ONES = cmat[:, 0, :]
    BONES = cmat[:, 1, :]
    RM = cmat[:, 2, :]
    RD = cmat[:, 3, :]
    EM = cmat[:, 4, :]

    sc = Sched()
    A = sc.add

    def gcol(name, idx, p0=0, p1=128):
        c = GC[name] + idx
        return gcols[p0:p1, c:c + 1]

    cnt = {"bank": 0}
    tf_free = list(range(NTF))
    tb_free = list(range(NTB))

    def tf():
        assert tf_free, "out of f32 temps"
        i = tf_free.pop(0)
        return tfb[:, i, :], ("tf", i)

    def tb():
        assert tb_free, "out of bf16 temps"
        i = tb_free.pop(0)
        return tbb[:, i, :], ("tb", i)

    def rel(*keys):
        for k in keys:
            (tf_free if k[0] == "tf" else tb_free).append(k[1])

    misc_banks = [list(range(8))]

    def bank():
        lst = misc_banks[0]
        i = lst[cnt["bank"] % len(lst)]
        cnt["bank"] += 1
        return ps[:, i, :], ("ps", i)

    def mm(out, lhsT, rhs, start, stop, reads, writes):
        A("pe", lambda e: e.matmul(out, lhsT, rhs, start=start, stop=stop), reads, writes)

    def act(out, in_, func, reads, writes, scale=1.0, bias=0.0):
        A("act", lambda e: e.activation(out=out, in_=in_, func=func, scale=scale, bias=bias), reads, writes)

    def stt(out, in0, scalar, in1, op0, op1, reads, writes, eng="dve"):
        A(eng, lambda e: e.scalar_tensor_tensor(out=out, in0=in0, scalar=scalar, in1=in1, op0=op0, op1=op1),
          reads, writes)

    def tt(out, in0, in1, op, reads, writes, eng="dve"):
        A(eng, lambda e: e.tensor_tensor(out=out, in0=in0, in1=in1, op=op), reads, writes)

    def ts(out, in0, s1, s2, op0, op1, reads, writes, eng="dve"):
        if s2 is None:
            A(eng, lambda e: e.tensor_scalar(out=out, in0=in0, scalar1=s1, scalar2=None, op0=op0), reads, writes)
        else:
            A(eng, lambda e: e.tensor_scalar(out=out, in0=in0, scalar1=s1, scalar2=s2, op0=op0, op1=op1),
              reads, writes)

    def vcopy(out, in_, reads, writes, eng="dve"):
        if eng == "act":
            act(out, in_, AF.Copy, reads, writes)
        else:
            A(eng, lambda e: e.tensor_copy(out=out, in_=in_), reads, writes)

    def recip(out, in_, reads, writes):
        act(out, in_, AF.Ln, reads, writes)
        act(out, out, AF.Exp, writes, writes, scale=-1.0)

    wstate = {"free": list(range(NWSLOT)), "pending": deque()}

    class WBlock:
        def __init__(self, src, n):
            self.src, self.n, self.slot, self.key = src, n, None, None

    def _w_issue(blk):
        slot = wstate["free"].pop(0)
        blk.slot = slot
        blk.key = ("w", slot)
        dst = Wr[:, slot, 0:blk.n]
        src = blk.src
        sc.dma("pool", lambda e: e.dma_start(out=dst, in_=src), "w%d" % slot, reads=(), writes=(blk.key,))

    def w_prefetch(src, n):
        blk = WBlock(src, n)
        if wstate["free"] and not wstate["pending"]:
            _w_issue(blk)
        else:
            wstate["pending"].append(blk)
        return blk

    def w_use(blk):
        assert blk.slot is not None, "weight block not issued (ring too small for this schedule)"
        return Wr[:, blk.slot, :], blk.key

    def w_release(blk):
        wstate["free"].append(blk.slot)
        while wstate["free"] and wstate["pending"]:
            _w_issue(wstate["pending"].popleft())

    sc.dma("sp", lambda e: e.dma_start(out=gcols[:, :], in_=gcols_d), "c0", writes=("gcols",))
    sc.dma("sp", lambda e: e.dma_start(out=ident[:, :], in_=ident_d), "c1", writes=("ident",))
    sc.dma("sp", lambda e: e.dma_start(out=ck[:, :], in_=ck_d), "c2", writes=("ck",))
    sc.dma("pool", lambda e: e.dma_start(out=cmat[:, :, :], in_=cmat_d.rearrange("p (a b) -> p a b", a=5)),
           "c3", writes=("cmat",))
    lam_v = TA[:, 0:4 * L * 64]
    sc.dma("sp", lambda e: e.dma_start(out=lam_v, in_=lamrep_d), "c4", writes=("TA",))
    lam4 = lam_v.rearrange("p (a l d) -> p a l d", a=4, l=L)
    pr = TB[:, 0:2 * L * 64].rearrange("p (a l d) -> p a l d", a=2, l=L)
    tt(pr[:, 0, :, :], lam4[:, 0, :, :], lam4[:, 1, :, :], ALU.mult, ("TA",), ("TB",))
    tt(pr[:, 1, :, :], lam4[:, 2, :, :], lam4[:, 3, :, :], ALU.mult, ("TA",), ("TB",))
    sums = lamc[:, 2 * L:4 * L]
    A("dve", lambda e: e.tensor_reduce(out=sums, in_=TB[:, 0:2 * L * 64].rearrange("p (a d) -> p a d", d=64),
                                       axis=AX.X, op=ALU.add), ("TB",), ("lamc",))
    act(sums, sums, AF.Exp, ("lamc",), ("lamc",))
    tt(lamc[:, 0:L], lamc[:, 2 * L:3 * L], lamc[:, 3 * L:4 * L], ALU.subtract, ("lamc",), ("lamc",))
    for l in range(L):
        li = lambda_init(l)
        ts(lamc[:, l:l + 1], lamc[:, l:l + 1], -1.0, -li, ALU.mult, ALU.add, ("lamc",), ("lamc",))
        ts(lamc[:, L + l:L + l + 1], gcol("gsub", l), 1.0 - li, None, ALU.mult, None, ("gcols", "lamc"), ("lamc",))

    def nlam(l):
        return lamc[:, l:l + 1]

    def gsubs(l):
        return lamc[:, L + l:L + l + 1]

    st_in = [tfb[:, 0:2, :], tfb[:, 2:4, :]]
    st_keys = [[("tf", 0), ("tf", 1)], [("tf", 2), ("tf", 3)]]

    chain_live = [0]
    maxc = [3]

    def chain(fill_A, P, nrm_n, onesmat, gname, gidx, buf, bkey, t, kind):
        tsl = slice(t * TT, (t + 1) * TT)
        while chain_live[0] >= maxc[0]:
            yield 0.7
        chain_live[0] += 1
        Ak, akey = bank()
        fill_A(Ak, akey)
        qf, qfk = tf()
        vcopy(qf[0:P, :], Ak[0:P, :], (akey,), (qfk,))
        sq, sqk = tb()
        if kind == "mla":
            tt(sq[0:P, :], qf[0:P, :], qf[0:P, :], ALU.mult, (qfk,), (sqk,), eng="pool")
        else:
            act(sq[0:P, :], qf[0:P, :], AF.Square, (qfk,), (sqk,))
        PR = 32 if kind == "mla" else P
        ts(buf[0:PR, tsl], qf[0:PR, :], gcol(gname, gidx, 0, PR), None, ALU.mult, None,
           (qfk, "gcols"), (bkey,))
        yield (4.0 if kind == "mla" else 3.0)
        Bk, bkey2 = bank()
        mm(Bk[0:P, :], onesmat[0:P, 0:P], sq[0:P, :], True, True, (sqk, "cmat"), (bkey2,))
        Ck, ckey = bank()
        if kind == "mla":
            mm(Ck[0:32, :], RM[0:32, 0:32], buf[0:32, tsl], True, True, (bkey, "cmat"), (ckey,))
            segs = ((0, 32, TA, 0, "TA", TA, 32),)
        else:
            mm(Ck[:, :], RD[:, :], buf[:, tsl], True, True, (bkey, "cmat"), (ckey,))
            segs = ((0, 16, TB, 0, "TB", TA, 96), (64, 16, TA, 64, "TA", TA, 96))
        t1, t1k = tf()
        for (p0, n, cosT, c0, ckn, sinT, s0) in segs:
            tt(t1[p0:p0 + n, :], Ck[p0:p0 + n, :], sinT[s0:s0 + n, tsl], ALU.mult, (ckey, "TA"), (t1k,))
        rt, rtk = tf()
        act(rt[0:P, :], Bk[0:P, :], AF.Ln, (bkey2,), (rtk,), scale=1.0 / nrm_n, bias=EPS)
        act(rt[0:P, :], rt[0:P, :], AF.Exp, (rtk,), (rtk,), scale=-0.5)
        stt(buf[0:P, tsl], qf[0:P, :], gcol(gname, gidx, 0, P), rt[0:P, :], ALU.mult, ALU.mult,
            (qfk, rtk, "gcols"), (bkey,))
        t0, t0k = tf()
        for (p0, n, cosT, c0, ckn, sinT, s0) in segs:
            tt(t1[p0:p0 + n, :], t1[p0:p0 + n, :], rt[p0:p0 + n, :], ALU.mult, (t1k, rtk), (t1k,), eng="pool")
            tt(t0[p0:p0 + n, :], buf[p0:p0 + n, tsl], cosT[c0:c0 + n, tsl], ALU.mult, (bkey, ckn), (t0k,), eng="pool")
            tt(buf[p0:p0 + n, tsl], t0[p0:p0 + n, :], t1[p0:p0 + n, :], ALU.add, (t0k, t1k), (bkey,), eng="pool")
        rel(qfk, sqk, rtk, t1k, t0k)
        chain_live[0] -= 1
        yield 0.0

    def norm_tile(l, gname, t, dst, dkey_fn, dst_t):
        tsl = slice(t * TT, (t + 1) * TT)
        dsl = slice(dst_t * TT, (dst_t + 1) * TT)
        Bk, bkey = bank()
        for kc in range(KC):
            sq, sqk = tb()
            act(sq[:, :], xT[:, kc, tsl], AF.Square, (("xT", kc, t),), (sqk,))
            mm(Bk[:, :], ONES, sq[:, :], kc == 0, kc == KC - 1, (sqk, "cmat"), (bkey,))
            rel(sqk)
        rt, rtk = tf()
        act(rt[:, :], Bk[:, :], AF.Ln, (bkey,), (rtk,), scale=1.0 / D, bias=EPS)
        act(rt[:, :], rt[:, :], AF.Exp, (rtk,), (rtk,), scale=-0.5)
        for kc in range(KC):
            stt(dst[:, kc, dsl], xT[:, kc, tsl], gcol(gname, l * 8 + kc), rt[:, :], ALU.mult, ALU.mult,
                (("xT", kc, t), rtk, "gcols"), (dkey_fn(kc, dst_t),))
        rel(rtk)

    hkey = lambda kc, t: ("hT", kc, t)

    def latents(l):
        blk = w_prefetch(WL_d[l], 8 * 416)
        wl, wkey = w_use(blk)
        wl3 = wl[:, 0:8 * 416].rearrange("p (k c) -> p k c", k=8)
        for t in range(NT):
            tsl = slice(t * TT, (t + 1) * TT)
            raws = []
            for c in range(3):
                Ak, akey = bank()
                for kc in range(KC):
                    mm(Ak[:, :], wl3[:, kc, c * 128:(c + 1) * 128], hT[:, kc, tsl], kc == 0, kc == KC - 1,
                       (wkey, hkey(kc, t)), (akey,))
                raws.append((Ak, akey))
            Ak, akey = bank()
            for kc in range(KC):
                mm(Ak[0:32, :], wl3[:, kc, 384:416], hT[:, kc, tsl], kc == 0, kc == KC - 1,
                   (wkey, hkey(kc, t)), (akey,))
            act(kpe[0:32, tsl], Ak[0:32, :], AF.Copy, (akey,), (("kpe", t),))
            B0, b0k = bank()
            B1, b1k = bank()
            for c in range(3):
                sq, sqk = tb()
                act(sq[:, :], raws[c][0][:, :], AF.Square, (raws[c][1],), (sqk,))
                if c < 2:
                    mm(B0[:, :], ONES, sq[:, :], c == 0, c == 1, (sqk, "cmat"), (b0k,))
                else:
                    mm(B1[:, :], ONES, sq[:, :], True, True, (sqk, "cmat"), (b1k,))
                rel(sqk)
            r0, r0k = tf()
            act(r0[:, :], B0[:, :], AF.Ln, (b0k,), (r0k,), scale=1.0 / 256, bias=EPS)
            act(r0[:, :], r0[:, :], AF.Exp, (r0k,), (r0k,), scale=-0.5)
            r1, r1k = tf()
            act(r1[:, :], B1[:, :], AF.Ln, (b1k,), (r1k,), scale=1.0 / 128, bias=EPS)
            act(r1[:, :], r1[:, :], AF.Exp, (r1k,), (r1k,), scale=-0.5)
            for c in range(2):
                stt(cqn[:, c, tsl], raws[c][0][:, :], gcol("gqa", l * 2 + c), r0[:, :], ALU.mult, ALU.mult,
                    (raws[c][1], r0k, "gcols"), (("cqn", c, t),))
            stt(ckvn[:, tsl], raws[2][0][:, :], gcol("gkva", l), r1[:, :], ALU.mult, ALU.mult,
                (raws[2][1], r1k, "gcols"), (("ckvn", t),))
            rel(r0k, r1k)
            for _ in range(8):
                bg_step()
                bg_tick(1.0)
        w_release(blk)

    bg = {"tasks": [], "now": 0.0, "seq": 0}

    def bg_add(gen, prio, delay=0.0):
        bg["tasks"].append([prio, bg["seq"], bg["now"] + delay, gen])
        bg["seq"] += 1

    def bg_step():
        while True:
            cands = [t for t in bg["tasks"] if t[2] <= bg["now"] + 1e-9]
            if not cands:
                return False
            t = min(cands, key=lambda t_: (t_[0], t_[1]))
            try:
                d = next(t[3])
                t[2] = bg["now"] + (d or 0.0)
                return True
            except StopIteration:
                bg["tasks"].remove(t)

    def bg_tick(dt):
        bg["now"] += dt

    def bg_finish(max_prio_left):
        while any(t[0] >= max_prio_left for t in bg["tasks"]):
            if not bg_step():
                bg_tick(0.5)

    def counted(g, state, on_done):
        for d in g:
            yield d
        state[0] -= 1
        if state[0] == 0 and on_done is not None:
            on_done()

    def prep_mla_tasks(l, h, slot, wm, wmkey):
        qs, ks, vs = slots[slot]
        wq = wm[:, 0:1536].rearrange("p (k h d) -> p k h d", k=2, h=8)
        wk = wm[:, 1536:2304].rearrange("p (h d) -> p h d", h=8)
        wv = wm[:, 2304:2816].rearrange("p (h d) -> p h d", h=8)
        voff = 0 if h % 2 == 0 else 64
        ooff = 64 - voff
        A("pool", lambda e: e.memset(vs[:, :, ooff:ooff + 64], 1.0), (), (("vs", slot),))

        def fq(t):
            def f(Ak, akey):
                tsl = slice(t * TT, (t + 1) * TT)
                for kc in range(2):
                    mm(Ak[0:96, :], wq[:, kc, h, :], cqn[:, kc, tsl], kc == 0, kc == 1,
                       (wmkey, ("cqn", kc, t)), (akey,))
            return f

        def fk(t):
            def f(Ak, akey):
                tsl = slice(t * TT, (t + 1) * TT)
                mm(Ak[0:96, :], wk[:, h, :], ckvn[:, tsl], True, False, (wmkey, ("ckvn", t)), (akey,))
                mm(Ak[0:96, :], EM[0:32, 0:96], kpe[0:32, tsl], False, True, ("cmat", ("kpe", t)), (akey,))
            return f

        def vgen(t):
            Vk, vkey = bank()
            for j in range(4):
                mm(Vk[:, j * 64:(j + 1) * 64], ckvn[:, t * TT + j * 128:t * TT + (j + 1) * 128], wv[:, h, :],
                   True, True, (wmkey, ("ckvn", t)), (vkey,))
            vcopy(vs[:, 4 * t:4 * t + 4, voff:voff + 64], Vk[:, 0:256].rearrange("p (a b) -> p a b", a=4),
                  (vkey,), (("vs", slot),))
            yield 0.0

        for t in range(NT):
            bg_add(chain(fq(t), 96, 96.0, ONES, "gq", l, qs, ("qs", slot, t), t, "mla"), 1)
            bg_add(chain(fk(t), 96, 96.0, ONES, "gk", l, ks, ("ks", slot, t), t, "mla"), 1)
            bg_add(vgen(t), 1)

    def prep_diff_tasks(l, j, slot):
        qs, ks, vs = slots[slot]
        blk = w_prefetch(WD_d[l, j], 8 * 384)
        state = [3 * NT]

        def wd3():
            wd, wkey = w_use(blk)
            return wd[:, 0:8 * 384].rearrange("p (k c) -> p k c", k=8), wkey

        def fqk(which, t):
            def f(Ak, akey):
                w3, wkey = wd3()
                tsl = slice(t * TT, (t + 1) * TT)
                for kc in range(KC):
                    mm(Ak[:, :], w3[:, kc, which * 128:(which + 1) * 128], hT[:, kc, tsl], kc == 0, kc == KC - 1,
                       (wkey, hkey(kc, t)), (akey,))
            return f

        def vgen(t):
            w3, wkey = wd3()
            Vk, vkey = bank()
            for j4 in range(4):
                for kc in range(KC):
                    mm(Vk[:, j4 * 128:(j4 + 1) * 128], hT[:, kc, t * TT + j4 * 128:t * TT + (j4 + 1) * 128],
                       w3[:, kc, 256:384], kc == 0, kc == KC - 1, (wkey, hkey(kc, t)), (vkey,))
            vcopy(vs[:, 4 * t:4 * t + 4, :], Vk[:, :].rearrange("p (a b) -> p a b", a=4), (vkey,), (("vs", slot),))
            yield 0.0

        rel_blk = lambda: w_release(blk)
        for t in range(NT):
            bg_add(counted(chain(fqk(0, t), 128, 64.0, BONES, "gdq", l, qs, ("qs", slot, t), t, "diff"),
                           state, rel_blk), 1)
            bg_add(counted(chain(fqk(1, t), 128, 64.0, BONES, "gdk", l, ks, ("ks", slot, t), t, "diff"),
                           state, rel_blk), 1)
            bg_add(counted(vgen(t), state, rel_blk), 1)

    att_cnt = {"pt": 0, "att": 0, "s": 0}
    busy = {}

    def att_alloc():
        ai = att_cnt["att"] % NATT
        att_cnt["att"] += 1
        assert not busy.get(("att", ai)), "att buffer still busy"
        busy[("att", ai)] = True
        return ai

    def wo_task(l, wo, wokey, krows, att_ap, attkey, qt, release_blk=None):
        p0, p1 = krows
        for m in range(KC):
            Yk, ykey = bank()
            mm(Yk[:, :], wo[p0:p1, m * 128:(m + 1) * 128], att_ap[p0:p1, :], True, True, (wokey, attkey), (ykey,))
            tsl = slice(qt * TT, (qt + 1) * TT)
            tt(xT[:, m, tsl], Yk[:, :], xT[:, m, tsl], ALU.add, (ykey, ("xT", m, qt)), (("xT", m, qt),))
            yield 0.0
        busy[attkey] = False
        if release_blk is not None:
            w_release(release_blk)

    ATTP = o1b[:, :, :].rearrange("p a b -> p (a b)").bitcast(BF16).rearrange("p (a b) -> p a b", a=4)

    def attention_head(kind, l, h, slot, scale, obanks, wo, wokey, release_blk):
        qs, ks, vs = slots[slot]
        nmap = 1 if kind == "mla" else 2
        Kr = 96 if kind == "mla" else 64
        dt_iter = 0.64 if kind == "mla" else 0.86
        iters = []
        for qt in range(NT):
            for mp in range(nmap):
                for kt in range(16):
                    iters.append((qt, mp, kt))
        LOOK = 2
        pend = deque()
        SB = (0, 1, 2)
        for i in range(len(iters) + LOOK):
            if i < len(iters):
                qt, mp, kt = iters[i]
                sbk = SB[att_cnt["s"] % 3]
                att_cnt["s"] += 1
                pti = att_cnt["pt"] % NPT
                att_cnt["pt"] += 1
                p0 = mp * 64 if kind == "diff" else 0
                qsl = slice(qt * TT, (qt + 1) * TT)
                mm(ps[:, sbk, :], ks[p0:p0 + Kr, kt * 128:(kt + 1) * 128], qs[p0:p0 + Kr, qsl], True, True,
                   (("ks", slot, kt // 4), ("qs", slot, qt)), (("ps", sbk),))
                act(PT[:, pti, :], ps[:, sbk, :], AF.Exp, (("ps", sbk),), (("pt", pti),), scale=scale)
                pend.append((qt, mp, kt, pti))
            if i >= LOOK:
                qt, mp, kt, pti = pend.popleft()
                if kind == "mla":
                    ob = obanks[qt % 2]
                    mm(ps[:, ob, :], vs[:, kt, :], PT[:, pti, :], kt == 0, kt == 15,
                       (("vs", slot), ("pt", pti)), (("ps", ob),))
                    if kt == 15:
                        orow = 0 if h % 2 == 0 else 64
                        srow = 64 - orow
                        rr, rrk = tf()
                        recip(rr[srow:srow + 64, :], ps[srow:srow + 64, ob, :], (("ps", ob),), (rrk,))
                        tt(ATTP[orow:orow + 64, qt, :], ps[orow:orow + 64, ob, :], rr[srow:srow + 64, :], ALU.mult,
                           (("ps", ob), rrk), (("attp", qt),))
                        rel(rrk)
                        last = (qt == NT - 1)
                        if h % 2 == 1:
                            bg_add(wo_task(l, wo, wokey, (0, 128), ATTP[:, qt, :], ("attp", qt), qt,
                                           release_blk if last else None), 0, delay=(3.0 if qt < 2 else 11.0))
                else:
                    ob, zb = obanks
                    mm(ps[:, ob, :], vs[:, kt, :], PT[:, pti, :], kt == 0, kt == 15,
                       (("vs", slot), ("pt", pti)), (("ps", ob),))
                    mm(ps[:, zb, :], ONES, PT[:, pti, :], kt == 0, kt == 15, (("pt", pti), "cmat"), (("ps", zb),))
                    if kt == 15:
                        oc, ock = tf()
                        vcopy(oc[:, :], ps[:, ob, :], (("ps", ob),), (ock,))
                        rr, rrk = tf()
                        recip(rr[:, :], ps[:, zb, :], (("ps", zb),), (rrk,))
                        dd, ddk = o1b[:, qt % 2, :], ("o1", qt % 2)
                        if mp == 0:
                            assert not busy.get(ddk), "o1 buffer still busy"
                            busy[ddk] = True
                            tt(dd, oc[:, :], rr[:, :], ALU.mult, (ock, rrk), (ddk,))
                            rel(rrk, ock)
                        else:
                            tt(oc[:, :], oc[:, :], rr[:, :], ALU.mult, (ock, rrk), (ock,))
                            stt(dd, oc[:, :], nlam(l), dd, ALU.mult, ALU.add, (ock, ddk, "lamc"), (ddk,))
                            rel(rrk, ock)
                            sq, sqk = tb()
                            act(sq[:, :], dd, AF.Square, (ddk,), (sqk,))
                            last = (qt == NT - 1)
                            bg_add(diff_post(l, dd, ddk, sq, sqk, qt, wo, wokey, release_blk if last else None), 0,
                                   delay=4.5)
            bg_step()
            bg_tick(dt_iter)

    def diff_post(l, dd, ddk, sq, sqk, qt, wo, wokey, release_blk):
        Bk, bkey = bank()
        mm(Bk[:, :], ONES, sq[:, :], True, True, (sqk, "cmat"), (bkey,))
        rt, rtk = tf()
        act(rt[:, :], Bk[:, :], AF.Ln, (bkey,), (rtk,), scale=1.0 / 128, bias=EPS)
        act(rt[:, :], rt[:, :], AF.Exp, (rtk,), (rtk,), scale=-0.5)
        ai = att_alloc()
        stt(ATT[:, ai, :], dd, gsubs(l), rt[:, :], ALU.mult, ALU.mult, (ddk, rtk, "lamc"), (("att", ai),))
        rel(rtk, sqk)
        busy[ddk] = False
        yield 0.0
        for d in wo_task(l, wo, wokey, (0, 128), ATT[:, ai, :], ("att", ai), qt, release_blk):
            yield d

    def attention_layer(l, pre_registered_diff0):
        wmb = None
        wm = wmkey = None
        if do_diff:
            misc_banks[0] = [5, 6, 7]
            if not pre_registered_diff0:
                prep_diff_tasks(l, 0, 0)
            bg_finish(1)
            for j in range(4):
                blk = w_prefetch(WO_d[l, 4 + j], 1024)
                if j + 1 < 4:
                    prep_diff_tasks(l, j + 1, (j + 1) % 2)
                elif do_mla:
                    wmb = w_prefetch(WM_d[l], 2816)
                    wm, wmkey = w_use(wmb)
                    prep_mla_tasks(l, 0, 0, wm, wmkey)
                wo, wokey = w_use(blk)
                attention_head("diff", l, j, j % 2, 1.0 / math.sqrt(64.0), (3, 4), wo, wokey, blk)
                bg_finish(1)
        if do_mla:
            bg_finish(0)
            w_ = [t for k in (("o1", 0), ("o1", 1)) for t in ([sc.w.get(k)] + list(sc.r.get(k, []))) if t is not None]
            sc.ops["dve"].append({"fn": None, "waits": [t for t in w_ if sc._need("dve", t, True)],
                                  "signal": False, "dma": None})
            misc_banks[0] = [4, 6, 7]
            if wmb is None:
                wmb = w_prefetch(WM_d[l], 2816)
                wm, wmkey = w_use(wmb)
                prep_mla_tasks(l, 0, 0, wm, wmkey)
                bg_finish(1)
            blk = None
            for h in range(8):
                if h % 2 == 0:
                    blk = w_prefetch(WO_d[l, h // 2], 1024)
                if h + 1 < 8:
                    prep_mla_tasks(l, h + 1, (h + 1) % 2, wm, wmkey)
                wo, wokey = w_use(blk)
                attention_head("mla", l, h, h % 2, 1.0 / math.sqrt(96.0), (3, 5), wo, wokey,
                               blk if h % 2 == 1 else None)
                bg_finish(1)
        bg_finish(0)
        if wmb is not None:
            w_release(wmb)
        misc_banks[0] = list(range(8))

    def ffn_layer(l):
        for hf in range(2):
            misc_banks[0] = [6, 7]
            if hf == 0:
                for tt_ in range(2):
                    norm_tile(l, "g2", tt_, hTh, lambda kc, t: ("hTh", kc, t), tt_)
            blks = deque()
            seq = [("gu", cp) for cp in range(11)] + [("dn", m) for m in range(8)]

            def pf(i):
                kind, idx = seq[i]
                if kind == "gu":
                    return w_prefetch(WGU_d[l, idx], 4096)
                return w_prefetch(WDN_d[l, idx], 2816)
            nxt = 0
            while nxt < min(2, len(seq)):
                blks.append(pf(nxt)); nxt += 1
            gub = 0
            for i, (kind, idx) in enumerate(seq):
                if hf == 0 and i == 11:
                    for tt_ in range(2):
                        norm_tile(l, "g2", 2 + tt_, hTh, lambda kc, t: ("hTh", kc, t), tt_)
                blk = blks.popleft()
                w, wkey = w_use(blk)
                if kind == "gu":
                    w4 = w[:, 0:4096].rearrange("p (a k c) -> p a k c", a=2, k=8)
                    for cc in range(2):
                        c = 2 * idx + cc
                        for tt_ in range(2):
                            gb = (gub % 2) * 2
                            gub += 1
                            dsl = slice(tt_ * TT, (tt_ + 1) * TT)
                            for a_ in range(2):
                                for kc in range(KC):
                                    mm(ps[:, gb + a_, :], w4[:, a_, kc, cc * 128:(cc + 1) * 128], hTh[:, kc, dsl],
                                       kc == 0, kc == KC - 1, (wkey, ("hTh", kc, tt_)), (("ps", gb + a_),))
                            sg, sgk = tf()
                            act(sg[:, :], ps[:, gb, :], AF.Silu, (("ps", gb),), (sgk,))
                            tt(actT[:, c, dsl], ps[:, gb + 1, :], sg[:, :], ALU.mult, (sgk, ("ps", gb + 1)),
                               (("actT", c, tt_),))
                            rel(sgk)
                else:
                    w3 = w[:, 0:2816].rearrange("p (c m) -> p c m", c=22)
                    for tt_ in range(2):
                        yb = 4 + (idx * 2 + tt_) % 2
                        dsl = slice(tt_ * TT, (tt_ + 1) * TT)
                        for c in range(NFC):
                            mm(ps[:, yb, :], w3[:, c, :], actT[:, c, dsl], c == 0, c == NFC - 1,
                               (wkey, ("actT", c, tt_)), (("ps", yb),))
                        t = 2 * hf + tt_
                        tsl = slice(t * TT, (t + 1) * TT)
                        tt(xT[:, idx, tsl], ps[:, yb, :], xT[:, idx, tsl], ALU.add,
                           (("ps", yb), ("xT", idx, t)), (("xT", idx, t),))
                w_release(blk)
                if nxt < len(seq):
                    blks.append(pf(nxt)); nxt += 1
        misc_banks[0] = list(range(8))

    def load_batch(b):
        scrF = U[:, 0:2 * S].bitcast(F32)
        scrI = U[:, 0:2 * S].bitcast(I32)
        C1 = 6.28125
        C2 = 2.0 * math.pi - C1
        sc.dma("sp", lambda e: e.dma_start(out=TB[:, :].bitcast(I32), in_=posrep[b]), "c5", writes=("TB",))
        vcopy(TA[:, :], TB[:, :].bitcast(I32), ("TB",), ("TA",))
        ts(TB[:, :], TA[:, :], ck[:, 2:3], ck[:, 3:4], ALU.mult, ALU.add, ("TA", "ck"), ("TB",))
        ts(TA[:, :], TA[:, :], ck[:, 0:1], ck[:, 1:2], ALU.mult, ALU.add, ("TA", "ck"), ("TA",))
        for (T_, k_) in ((TA, "TA"), (TB, "TB")):
            ts(scrF, T_[:, :], 1.0 / (2.0 * math.pi), None, ALU.mult, None, (k_,), ("Uscr",))
            vcopy(scrI, scrF, ("Uscr",), ("Uscr",))
            vcopy(scrF, scrI, ("Uscr",), ("Uscr",))
            stt(T_[:, :], scrF, -C1, T_[:, :], ALU.mult, ALU.add, ("Uscr", k_), (k_,))
            stt(T_[:, :], scrF, -C2, T_[:, :], ALU.mult, ALU.add, ("Uscr", k_), (k_,))
            ts(scrF, T_[:, :], math.pi, 2.0 * math.pi, ALU.is_gt, ALU.mult, (k_,), ("Uscr",))
            tt(T_[:, :], T_[:, :], scrF, ALU.subtract, (k_, "Uscr"), (k_,))
            ts(T_[:, :], T_[:, :], -math.pi, math.pi, ALU.max, ALU.min, (k_,), (k_,))
            act(T_[:, :], T_[:, :], AF.Sin, (k_,), (k_,))
        for i in range(S // 128):
            si = i % 2
            stg = st_in[si]
            sc.dma("sp", lambda e, stg=stg, i=i: e.dma_start(
                out=stg, in_=xin[b, i * 128:(i + 1) * 128, :].rearrange("p (a c) -> p a c", a=2)),
                "st%d" % si, writes=tuple(st_keys[si]))
            stf = [tfb[:, 2 * si, :], tfb[:, 2 * si + 1, :]]
            for half in range(2):
                Bk, bkey = bank()
                for q4 in range(4):
                    kc = half * 4 + q4
                    src = stf[kc // 4][:, (kc % 4) * 128:(kc % 4 + 1) * 128]
                    A("pe", lambda e, o_=Bk[:, q4 * 128:(q4 + 1) * 128], s_=src: e.transpose(o_, s_, ident[:, :]),
                      (st_keys[si][kc // 4], "ident"), (bkey,))
                t = i // 4
                vcopy(xT[:, half * 4:half * 4 + 4, i * 128:(i + 1) * 128],
                      Bk[:, :].rearrange("p (a c) -> p a c", a=4), (bkey,),
                      tuple(("xT", half * 4 + q, t) for q in range(4)), eng=("dve" if half == 0 else "act"))

    def store_batch(b):
        for i in range(S // 128):
            si = i % 2
            stf = [tfb[:, 2 * si, :], tfb[:, 2 * si + 1, :]]
            t = i // 4
            for half in range(2):
                Bk, bkey = bank()
                for q4 in range(4):
                    kc = half * 4 + q4
                    A("pe", lambda e, o_=Bk[:, q4 * 128:(q4 + 1) * 128], s_=xT[:, kc, i * 128:(i + 1) * 128]:
                      e.transpose(o_, s_, ident[:, :]), (("xT", kc, t), "ident"), (bkey,))
                vcopy(stf[half][:, :], Bk[:, :], (bkey,), (st_keys[si][half],), eng=("dve" if half == 0 else "act"))
            stg = st_in[si]
            sc.dma("sp", lambda e, stg=stg, i=i: e.dma_start(
                out=yout[b, i * 128:(i + 1) * 128, :].rearrange("p (a c) -> p a c", a=2), in_=stg),
                "so%d" % si, reads=tuple(st_keys[si]))

    for b in range(NB):
        sc.barrier()
        load_batch(b)
        for l in range(L):
            sc.barrier()
            misc_banks[0] = list(range(8))
            for t in range(NT):
                norm_tile(l, "g1", t, hT, hkey, t)
            pre = False
            if do_diff:
                prep_diff_tasks(l, 0, 0)
                pre = True
            if do_mla:
                latents(l)
            attention_layer(l, pre)
            sc.barrier()
            if do_ffn:
                ffn_layer(l)
        sc.barrier()
        store_batch(b)
    sc.wait_all_dma("sp", ["so0", "so1"])
    sc.emit(nc)
    return nc, sc


def make_in_maps(inp, L, ncores, nb):
    shared = host_prepare(inp, L)
    x = np.ascontiguousarray(np.asarray(inp["x"]), dtype=np.float32)
    pos = np.ascontiguousarray(np.asarray(inp["positions"]), dtype=np.int32)
    in_maps = []
    for c in range(ncores):
        m = dict(shared)
        m["xin"] = np.ascontiguousarray(x[c * nb:(c + 1) * nb])
        m["posrep"] = np.ascontiguousarray(np.broadcast_to(pos[c * nb:(c + 1) * nb, None, :], (nb, 128, S)))
        in_maps.append(m)
    return in_maps


_CACHE = {}


def kernel(**inputs):
    key = "full"
    if key not in _CACHE:
        _CACHE[key] = build_program(L_FULL, B_PER_CORE)[0]
    nc = _CACHE[key]
    in_maps = make_in_maps(inputs, L_FULL, NCORES, B_PER_CORE)
    res = run_bass_kernel_spmd(nc, in_maps, core_ids=list(range(NCORES)))
    out = np.concatenate([np.asarray(r["yout"]) for r in res.results], axis=0)
    return out.astype(np.float32, copy=False)
```
